# Optimizing a Trainium2 kernel written in Bass

```python
import jax, jax.numpy as jnp
from jax import lax
import numpy as np

D_MODEL = 4096
BATCH = 4
SEQ = 2048
DEPTH = 1

CONV_WIDTH = D_MODEL
CONV_K = 3
SSM_EXPAND = 2
SSM_D_INNER = SSM_EXPAND * D_MODEL
SSM_HEAD_DIM = 64
SSM_N_HEADS = SSM_D_INNER // SSM_HEAD_DIM
SSM_N_GROUPS = 8
SSM_D_STATE = 128
SSM_CONV_K = 4
SSM_CHUNK = 128
SSM_CONV_DIM = SSM_D_INNER + 2 * SSM_N_GROUPS * SSM_D_STATE
D_FF = 11008
FFN_RESIDUAL_SCALE = 0.5
RMS_EPS = 1e-5

IN_SPLITS = (CONV_WIDTH, CONV_WIDTH, CONV_WIDTH,
             SSM_D_INNER,
             SSM_CONV_DIM,
             SSM_N_HEADS,
             D_MODEL, D_MODEL)
D_IN_PROJ = CONV_WIDTH * 3 + SSM_D_INNER + SSM_CONV_DIM + SSM_N_HEADS + 2 * D_MODEL

kernel_name = "hybrid_conv_ssd_gated_macaron"


def _split_cols(t, sizes):
    out, start = [], 0
    for s in sizes:
        out.append(t[..., start:start + s])
        start += s
    return out


def _rmsnorm(x, w):
    x32 = x.astype(jnp.float32)
    y = x32 * lax.rsqrt(jnp.mean(x32 * x32, axis=-1, keepdims=True) + RMS_EPS)
    return (y * w.astype(jnp.float32)).astype(x.dtype)


def _swiglu(h, w_gate, w_up, w_down):
    return (jax.nn.silu(h @ w_gate) * (h @ w_up)) @ w_down


def _causal_depthwise_conv(u, w):
    k_taps = w.shape[0]
    length = u.shape[1]
    up = jnp.pad(u, ((0, 0), (k_taps - 1, 0), (0, 0)))
    y = up[:, 0:length] * w[0]
    for k in range(1, k_taps):
        y = y + up[:, k:k + length] * w[k]
    return y


def _ssd_chunked(xs, dt, a_neg, bm, cm):
    b, l, h, p = xs.shape
    g, n = bm.shape[2], bm.shape[3]
    r = h // g
    c = l // SSM_CHUNK
    T = SSM_CHUNK
    xdt = (xs * dt[..., None]).reshape(b, c, T, g, r, p)
    a = (dt * a_neg).reshape(b, c, T, g, r)
    a_cs = jnp.cumsum(a, axis=2)
    bc = bm.reshape(b, c, T, g, n)
    cc = cm.reshape(b, c, T, g, n)

    a_t = jnp.transpose(a_cs, (0, 1, 3, 4, 2))
    seg = a_t[..., :, None] - a_t[..., None, :]
    causal = jnp.tril(jnp.ones((T, T), dtype=bool))
    lmat = jnp.exp(jnp.where(causal, seg, -jnp.inf))
    cb = jnp.einsum('bcign,bcjgn->bcgij', cc, bc)
    scores = cb[:, :, :, None] * lmat
    y_diag = jnp.einsum('bcgrij,bcjgrp->bcigrp', scores, xdt)

    decay_to_end = jnp.exp(a_cs[:, :, -1:] - a_cs)
    states = jnp.einsum('bcjgn,bcjgrp->bcgrpn', bc, xdt * decay_to_end[..., None])

    chunk_decay = jnp.exp(a_cs[:, :, -1])

    def step(carry, inp):
        s_c, d_c = inp
        new = d_c[..., None, None] * carry + s_c
        return new, carry

    init = jnp.zeros((b, g, r, p, n), dtype=xs.dtype)
    _, prev_states = lax.scan(step, init, (jnp.moveaxis(states, 1, 0), jnp.moveaxis(chunk_decay, 1, 0)))
    prev_states = jnp.moveaxis(prev_states, 0, 1)

    y_off = jnp.einsum('bcign,bcgrpn->bcigrp', cc, prev_states) * jnp.exp(a_cs)[..., None]
    return (y_diag + y_off).reshape(b, l, h, p)


def _mixer_block(h, w_in, gate_bias, sconv_w, sconv_w_out, ssm_conv_w, ssm_conv_b,
                 ssm_dt_bias, ssm_A_log, ssm_D, ssm_norm, ssm_w_out, w_o):
    bsz, length, _ = h.shape
    proj = jnp.einsum('bld,de->ble', h, w_in)
    c_b, c_c, c_x, z, xbc, dt_raw, g_a, g_b = _split_cols(proj, IN_SPLITS)

    y_a = (c_b * _causal_depthwise_conv(c_c * c_x, sconv_w)) @ sconv_w_out

    xbc = jax.nn.silu(_causal_depthwise_conv(xbc, ssm_conv_w) + ssm_conv_b)
    xs, bm, cm = _split_cols(xbc, (SSM_D_INNER, SSM_N_GROUPS * SSM_D_STATE, SSM_N_GROUPS * SSM_D_STATE))
    xs = xs.reshape(bsz, length, SSM_N_HEADS, SSM_HEAD_DIM).astype(jnp.float32)
    bm = bm.reshape(bsz, length, SSM_N_GROUPS, SSM_D_STATE).astype(jnp.float32)
    cm = cm.reshape(bsz, length, SSM_N_GROUPS, SSM_D_STATE).astype(jnp.float32)
    dt = jax.nn.softplus(dt_raw.astype(jnp.float32) + ssm_dt_bias.astype(jnp.float32))
    a_neg = -jnp.exp(ssm_A_log.astype(jnp.float32))
    y = _ssd_chunked(xs, dt, a_neg, bm, cm) + ssm_D.astype(jnp.float32)[:, None] * xs
    y = y.reshape(bsz, length, SSM_D_INNER) * jax.nn.silu(z.astype(jnp.float32))
    yg = y.reshape(bsz, length, SSM_N_GROUPS, SSM_D_INNER // SSM_N_GROUPS)
    yg = yg * lax.rsqrt(jnp.mean(yg * yg, axis=-1, keepdims=True) + RMS_EPS)
    y = (yg.reshape(bsz, length, SSM_D_INNER) * ssm_norm.astype(jnp.float32)).astype(h.dtype)
    y_b = y @ ssm_w_out

    gates = jax.nn.sigmoid(jnp.concatenate([g_a, g_b], axis=-1) + gate_bias)
    gate_a, gate_b = _split_cols(gates, (D_MODEL, D_MODEL))
    return (gate_a * y_a + gate_b * y_b) @ w_o


def setup_inputs(seed: int = 0) -> dict:
    key = jax.random.key(seed)
    ks = jax.random.split(key, 24)
    f32 = jnp.float32
    nrm = lambda k, shape, scale: jax.random.normal(k, shape, f32) * scale
    gain = lambda k, shape: 1.0 + 0.02 * jax.random.normal(k, shape, f32)
    L = DEPTH
    dt_init = jnp.exp(jax.random.uniform(ks[11], (L, SSM_N_HEADS), f32, np.log(1e-3), np.log(1e-1)))
    return {
        "x": jax.random.normal(ks[0], (BATCH, SEQ, D_MODEL), f32),
        "ffn1_norm": gain(ks[1], (L, D_MODEL)),
        "ffn1_w_gate": nrm(ks[2], (L, D_MODEL, D_FF), D_MODEL ** -0.5),
        "ffn1_w_up": nrm(ks[3], (L, D_MODEL, D_FF), D_MODEL ** -0.5),
        "ffn1_w_down": nrm(ks[4], (L, D_FF, D_MODEL), D_FF ** -0.5),
        "mix_norm": gain(ks[5], (L, D_MODEL)),
        "w_in": nrm(ks[6], (L, D_MODEL, D_IN_PROJ), D_MODEL ** -0.5),
        "gate_bias": nrm(ks[7], (L, 2 * D_MODEL), 0.01),
        "sconv_w": nrm(ks[8], (L, CONV_K, CONV_WIDTH), CONV_K ** -0.5),
        "sconv_w_out": nrm(ks[9], (L, CONV_WIDTH, D_MODEL), CONV_WIDTH ** -0.5),
        "ssm_conv_w": nrm(ks[10], (L, SSM_CONV_K, SSM_CONV_DIM), SSM_CONV_K ** -0.5),
        "ssm_conv_b": nrm(ks[12], (L, SSM_CONV_DIM), 0.01),
        "ssm_dt_bias": dt_init + jnp.log(-jnp.expm1(-dt_init)),
        "ssm_A_log": jnp.log(jax.random.uniform(ks[13], (L, SSM_N_HEADS), f32, 1.0, 16.0)),
        "ssm_D": 1.0 + 0.1 * jax.random.normal(ks[14], (L, SSM_N_HEADS), f32),
        "ssm_norm": gain(ks[15], (L, SSM_D_INNER)),
        "ssm_w_out": nrm(ks[16], (L, SSM_D_INNER, D_MODEL), SSM_D_INNER ** -0.5),
        "w_o": nrm(ks[17], (L, D_MODEL, D_MODEL), D_MODEL ** -0.5),
        "ffn2_norm": gain(ks[18], (L, D_MODEL)),
        "ffn2_w_gate": nrm(ks[19], (L, D_MODEL, D_FF), D_MODEL ** -0.5),
        "ffn2_w_up": nrm(ks[20], (L, D_MODEL, D_FF), D_MODEL ** -0.5),
        "ffn2_w_down": nrm(ks[21], (L, D_FF, D_MODEL), D_FF ** -0.5),
        "final_norm": gain(ks[22], (D_MODEL,)),
    }


def reference(x, ffn1_norm, ffn1_w_gate, ffn1_w_up, ffn1_w_down, mix_norm, w_in, gate_bias,
              sconv_w, sconv_w_out, ssm_conv_w, ssm_conv_b, ssm_dt_bias, ssm_A_log, ssm_D,
              ssm_norm, ssm_w_out, w_o, ffn2_norm, ffn2_w_gate, ffn2_w_up, ffn2_w_down,
              final_norm):
    for i in range(DEPTH):
        x = x + FFN_RESIDUAL_SCALE * _swiglu(_rmsnorm(x, ffn1_norm[i]), ffn1_w_gate[i], ffn1_w_up[i], ffn1_w_down[i])
        h = _rmsnorm(x, mix_norm[i])
        x = x + _mixer_block(h, w_in[i], gate_bias[i], sconv_w[i], sconv_w_out[i], ssm_conv_w[i],
                             ssm_conv_b[i], ssm_dt_bias[i], ssm_A_log[i], ssm_D[i], ssm_norm[i],
                             ssm_w_out[i], w_o[i])
        x = x + FFN_RESIDUAL_SCALE * _swiglu(_rmsnorm(x, ffn2_norm[i]), ffn2_w_gate[i], ffn2_w_up[i], ffn2_w_down[i])
    return _rmsnorm(x, final_norm)
```

```python
from contextlib import ExitStack
import numpy as np
import concourse.bass as bass
import concourse.mybir as mybir
from concourse.bass_utils import run_bass_kernel_spmd

F32 = mybir.dt.float32
BF16 = mybir.dt.bfloat16
AF = mybir.ActivationFunctionType
ALU = mybir.AluOpType

D = 4096
DFF = 11008
NFF = 86
SEQ = 2048
NT = 512
EPS = 1e-5
O_CB, O_CC, O_CX, O_Z, O_XBC, O_DT, O_GA, O_GB = 0, 4096, 8192, 12288, 20480, 30720, 30848, 34944
CV_N1, CV_NM, CV_N2, CV_NF, CV_GB, CV_SCW, CV_CW, CV_CB, CV_SN, NCV = 0, 32, 64, 96, 128, 192, 288, 608, 688, 752

ENGS = ("pe", "act", "dve", "pool", "sp")
RING = {"pool": 8, "sp": 16, "act": 8}


class Tile:
    __slots__ = ("name", "w", "wd", "r", "rd", "x", "xl")

    def __init__(self, name, excl=False):
        self.name = name
        self.x = excl
        self.xl = {}
        self.w = {}
        self.wd = []
        self.r = {}
        self.rd = []


class Op:
    __slots__ = ("eng", "fn", "is_dma", "deps", "signal", "tok_val", "sem", "idx")


class Prog:
    def __init__(self, nc, stack):
        self.nc = nc
        self.ops = {e: [] for e in ENGS}
        self.n = 0
        self.esem = {e: stack.enter_context(nc.semaphore("es_" + e)) for e in ENGS}
        self.ring = {}
        for q, n in RING.items():
            self.ring[q] = [[stack.enter_context(nc.semaphore("rs_%s%d" % (q, i))), 0, None] for i in range(n)]
        self.rpos = {q: 0 for q in RING}
        self.bar = []
        self.bar_seen = set()

    def barrier(self):
        last = []
        for e in ("pe", "act", "dve"):
            for op in reversed(self.ops[e]):
                if (not op.is_dma) and op.fn is not None:
                    last.append(op)
                    break
        for q in ("sp", "act"):
            for slot in self.ring[q]:
                if slot[2] is not None:
                    last.append(slot[2])
        self.bar = last
        self.bar_seen = set()

    def add(self, eng, fn, reads=(), writes=(), dma=False):
        op = Op()
        op.eng, op.fn, op.is_dma, op.signal, op.tok_val, op.sem = eng, fn, dma, False, None, None
        op.idx = self.n
        self.n += 1
        deps = {}
        for t in reads:
            for w in t.w.values():
                deps[w.idx] = w
            for w in t.wd:
                deps[w.idx] = w
        for t in writes:
            for w in t.w.values():
                deps[w.idx] = w
            for w in t.wd:
                deps[w.idx] = w
            for r in t.r.values():
                deps[r.idx] = r
            for r in t.rd:
                deps[r.idx] = r
        for t in list(reads) + list(writes):
            if t.x:
                for e2, o2 in t.xl.items():
                    if e2 != eng:
                        deps[o2.idx] = o2
                t.xl[eng] = op
        if eng != "pool" and self.bar and eng not in self.bar_seen:
            self.bar_seen.add(eng)
            for b in self.bar:
                deps[b.idx] = b
        if dma:
            slot = self.ring[eng][self.rpos[eng] % len(self.ring[eng])]
            self.rpos[eng] += 1
            if slot[2] is not None:
                deps[slot[2].idx] = slot[2]
            slot[1] += 16
            slot[2] = op
            op.sem, op.tok_val = slot[0], slot[1]
        for t in reads:
            if dma:
                t.rd.append(op)
            else:
                t.r[eng] = op
        for t in writes:
            t.w, t.wd, t.r, t.rd = {}, [], {}, []
            if dma:
                t.wd.append(op)
            else:
                t.w[eng] = op
        dl = []
        for d in deps.values():
            if (not d.is_dma) and (not dma) and d.eng == "pe" and eng == "pe":
                continue
            if not d.is_dma:
                d.signal = True
            dl.append(d)
        op.deps = dl
        self.ops[eng].append(op)
        return op

    def emit(self, final_tiles=()):
        nc = self.nc
        self.add("sp", None, reads=list(final_tiles))
        for e in ENGS:
            c = 0
            for op in self.ops[e]:
                if (not op.is_dma) and op.signal:
                    c += 1
                    op.tok_val = c
        prog = self

        def run(e, eng):
            waited = {}
            for op in prog.ops[e]:
                for d in op.deps:
                    sem, val = (d.sem, d.tok_val) if d.is_dma else (prog.esem[d.eng], d.tok_val)
                    k = id(sem)
                    if waited.get(k, 0) >= val:
                        continue
                    waited[k] = val
                    eng.wait_ge(sem, val)
                if op.fn is None:
                    continue
                inst = op.fn(eng)
                if op.is_dma:
                    inst.then_inc(op.sem, 16)
                elif op.signal:
                    inst.then_inc(prog.esem[e], 1)

        with nc.Block() as block:
            @block.tensor
            def _(eng):
                run("pe", eng)

            @block.scalar
            def _(eng):
                run("act", eng)

            @block.vector
            def _(eng):
                run("dve", eng)

            @block.gpsimd
            def _(eng):
                run("pool", eng)

            @block.sync
            def _(eng):
                run("sp", eng)


class SB:
    def __init__(self, h, F):
        self.h, self.F = h, F

    def __getitem__(self, key):
        return self.h[key]

    def v(self, off, dims, npart=128):
        return bass.AP(self.h, off, [[self.F, npart]] + [list(d) for d in dims])


def unit_list(mode="full"):
    u = []

    def ffn(i):
        for m in range(NFF):
            u.append(("g%d" % i, m * 128, 0, 32))
            u.append(("u%d" % i, m * 128, 0, 32))
        for m in range(32):
            u.append(("d%d" % i, m * 128, 0, 32))
            u.append(("d%d" % i, m * 128, 32, 32))
            u.append(("d%d" % i, m * 128, 64, 22))

    ffn(1)
    if mode == "full":
        for c in range(32):
            u.append(("win", O_CC + c * 128, 0, 32))
            u.append(("win", O_CX + c * 128, 0, 32))
            u.append(("win", O_CB + c * 128, 0, 32))
        for c in range(32):
            u.append(("win", O_GA + c * 128, 0, 32))
            u.append(("sco", c * 128, 0, 32))
    else:
        for c in range(32):
            u.append(("win", O_CC + c * 128, 0, 32))
            u.append(("win", O_CX + c * 128, 0, 32))
    for c in range(64, 80):
        u.append(("win", O_XBC + c * 128, 0, 32))
    u.append(("win", O_DT, 0, 32))
    for c in range(64):
        u.append(("win", O_XBC + c * 128, 0, 32))
    if mode == "full":
        for c in range(64):
            u.append(("win", O_Z + c * 128, 0, 32))
        for c in range(32):
            u.append(("win", O_GB + c * 128, 0, 32))
            u.append(("sso", c * 128, 0, 32))
            u.append(("sso", c * 128, 32, 32))
        for c in range(32):
            u.append(("wo", c * 128, 0, 32))
        ffn(2)
    return u


def build(modes, debug=False, upto=99):
    ntile = len(modes)
    nfull = sum(1 for m in modes if m == "full")
    ulists = {m: unit_list(m) for m in set(modes)}
    ubase = {}
    nu = 0
    for m in sorted(ulists):
        ubase[m] = nu
        nu += len(ulists[m])

    nc = bass.Bass("TRN2", target_bir_lowering=False)
    x_d = nc.dram_tensor("x", [ntile * NT, D], F32, kind="ExternalInput").ap()
    ws_d = nc.dram_tensor("ws", [nu, 128, 4096], F32, kind="ExternalInput").ap()
    cv_d = nc.dram_tensor("cv", [128, NCV], F32, kind="ExternalInput").ap()
    rb_d = nc.dram_tensor("rb", [128, 384], F32, kind="ExternalInput").ap()
    sm_d = nc.dram_tensor("sm", [128, 1], F32, kind="ExternalInput").ap()
    out_d = nc.dram_tensor("out", [nfull * NT, D], F32, kind="ExternalOutput").ap()
    xT_d = nc.dram_tensor("xT_s", [32, 128, NT], F32).ap()
    mA_d = nc.dram_tensor("mA_s", [32, 128, NT], F32).ap()
    yT_d = nc.dram_tensor("yT_s", [64, 128, NT], F32).ap()
    S_d = nc.dram_tensor("S_s", [8, 128, 1024], F32).ap()
    dbg_d = None
    if debug:
        dbg_d = nc.dram_tensor("dbg", [4, 32, 128, NT], F32, kind="ExternalOutput").ap()
        dby_d = nc.dram_tensor("dby", [64, 128, NT], F32, kind="ExternalOutput").ap()

    with ExitStack() as st:
        P = Prog(nc, st)

        def sb(name, F, dt):
            return SB(st.enter_context(nc.sbuf_tensor("s_" + name, [128, F], dt)), F)

        def pst(name, F, dt):
            return SB(st.enter_context(nc.psum_tensor("p_" + name, [128, F], dt)), F)

        cv = sb("cv", NCV, F32)
        rb = sb("rb", 384, F32)
        smk = sb("smk", 1, F32)
        identf = sb("identf", 128, F32)
        identb = sb("identb", 128, BF16)
        tri = sb("tri", 128, F32)
        maskf = sb("maskf", 128, F32)
        onesf = sb("onesf", 128, F32)
        onesb = sb("onesb", 128, BF16)
        Ab = sb("Ab", 128, F32)
        xh = sb("xh", 80 * 3, F32)
        uh = sb("uh", 32 * 2, F32)
        NSLOT = 5
        wsl = [sb("wsl%d" % i, 4096, BF16) for i in range(NSLOT)]
        hT = sb("hT", 32 * NT, BF16)
        ARENA = 65024
        ar = sb("arena", ARENA, BF16)
        arf = SB(ar.h, ARENA)

        def arena_view(off_bytes, nelem, dt):
            if dt == BF16:
                a = ar[:, off_bytes // 2: off_bytes // 2 + nelem]
            else:
                a = ar[:, off_bytes // 2: off_bytes // 2 + nelem * 2].bitcast(F32)
            return a

        psb = [pst("ps%d" % i, 512, F32) for i in range(7)]
        pstb = pst("pstb", 1024, BF16)
        t_ps = [Tile("ps%d" % i, True) for i in range(7)]
        _tp = Tile("pstb", True)
        t_pstb = [_tp, _tp]

        t_cv, t_rb, t_const = Tile("cv"), Tile("rb"), Tile("const")
        t_wsl = [Tile("wsl%d" % i) for i in range(NSLOT)]
        t_hT = [Tile("hT%d" % i) for i in range(32)]
        t_xT = [Tile("xT%d" % i) for i in range(32)]
        t_mA = [Tile("mA%d" % i) for i in range(32)]
        t_yT = [Tile("yT%d" % i) for i in range(64)]
        t_S = [Tile("S%d" % i) for i in range(8)]
        t_xh = [Tile("xh%d" % i) for i in range(80)]
        t_uh = [Tile("uh%d" % i) for i in range(32)]
        t_out = []

        def mm(ps, lhsT, rhs, start, stop, reads, writes):
            P.add("pe", lambda e: e.matmul(ps, lhsT=lhsT, rhs=rhs, start=start, stop=stop), reads=reads, writes=writes)

        def tr(ps, in_, ident, reads, writes):
            P.add("pe", lambda e: e.transpose(ps, in_, ident), reads=reads, writes=writes)

        def act(out, in_, func, reads, writes, bias=0.0, scale=1.0):
            P.add("act", lambda e: e.activation(out=out, in_=in_, func=func, bias=bias, scale=scale), reads=reads, writes=writes)

        def tt(out, in0, in1, op, reads, writes, eng="dve"):
            P.add(eng, lambda e: e.tensor_tensor(out=out, in0=in0, in1=in1, op=op), reads=reads, writes=writes)

        def ts(out, in0, s1, s2, op0, op1, reads, writes, eng="dve"):
            P.add(eng, lambda e: e.tensor_scalar(out=out, in0=in0, scalar1=s1, scalar2=s2, op0=op0, op1=op1), reads=reads, writes=writes)

        def ts1(out, in0, s1, op0, reads, writes, eng="dve"):
            P.add(eng, lambda e: e.tensor_single_scalar(out=out, in_=in0, scalar=s1, op=op0), reads=reads, writes=writes)

        def stt(out, in0, scalar, in1, op0, op1, reads, writes, eng="dve"):
            P.add(eng, lambda e: e.scalar_tensor_tensor(out=out, in0=in0, scalar=scalar, in1=in1, op0=op0, op1=op1), reads=reads, writes=writes)

        def cp(out, in_, reads, writes, eng="dve"):
            if eng == "act":
                P.add("act", lambda e: e.copy(out=out, in_=in_), reads=reads, writes=writes)
            else:
                P.add(eng, lambda e: e.tensor_copy(out=out, in_=in_), reads=reads, writes=writes)

        def dma(q, out, in_, reads, writes):
            P.add(q, lambda e: e.dma_start(out=out, in_=in_), reads=reads, writes=writes, dma=True)

        dma("sp", cv[:, :], cv_d, [], [t_cv])
        dma("sp", rb[:, :], rb_d, [], [t_rb])
        dma("sp", smk[:, :], sm_d, [], [t_rb])
        cst = [t_const]
        P.add("pool", lambda e: e.memset(identf[:, :], 1.0), writes=cst)
        P.add("pool", lambda e: e.affine_select(out=identf[:, :], in_=identf[:, :], pattern=[[1, 128]], compare_op=ALU.is_equal,
                                                fill=0.0, base=0, channel_multiplier=-1), reads=cst, writes=cst)
        P.add("pool", lambda e: e.tensor_copy(out=identb[:, :], in_=identf[:, :]), reads=cst, writes=cst)
        P.add("pool", lambda e: e.memset(tri[:, :], 1.0), reads=cst, writes=cst)
        P.add("pool", lambda e: e.affine_select(out=tri[:, :], in_=tri[:, :], pattern=[[1, 128]], compare_op=ALU.is_ge,
                                                fill=0.0, base=0, channel_multiplier=-1), reads=cst, writes=cst)
        P.add("pool", lambda e: e.memset(maskf[:, :], 0.0), reads=cst, writes=cst)
        P.add("pool", lambda e: e.affine_select(out=maskf[:, :], in_=maskf[:, :], pattern=[[1, 128]], compare_op=ALU.is_ge,
                                                fill=-30000.0, base=0, channel_multiplier=-1), reads=cst, writes=cst)
        P.add("pool", lambda e: e.memset(onesf[:, :], 1.0), reads=cst, writes=cst)
        P.add("pool", lambda e: e.memset(onesb[:, :], 1.0), reads=cst, writes=cst)
        P.add("pool", lambda e: e.memset(xh[:, :], 0.0), writes=t_xh)
        P.add("pool", lambda e: e.memset(uh[:, :], 0.0), writes=t_uh)
        act(Ab[:, :], rb[:, 128:256], AF.Exp, [t_rb], cst)
        ts1(Ab[:, :], Ab[:, :], -1.0, ALU.mult, cst, cst)
        dtb_b = rb[:, 0:128]
        D_b = rb[:, 256:384]
        CRD = [t_cv, t_rb, t_const]

        wstate = {"slot": 0, "cur": None, "i": 0}

        def next_unit(expect_name, expect_col, expect_k0):
            ul, base, i = wstate["cur"]
            name, col0, k0, kc = ul[wstate["i"]]
            assert (name, col0, k0) == (expect_name, expect_col, expect_k0), (name, col0, k0, expect_name, expect_col, expect_k0)
            s = wstate["slot"] % NSLOT
            wstate["slot"] += 1
            uidx = base + wstate["i"]
            wstate["i"] += 1
            dma("pool", wsl[s][:, 0:kc * 128], ws_d[uidx][:, 0:kc * 128], [], [t_wsl[s]])
            return wsl[s], t_wsl[s], kc

        def gemm(ps, pt, units, rhs_of_k, rt_of_k, extra_reads=()):
            loaded = []
            tot = 0
            for (name, col0, k0) in units:
                w, wt, kc = next_unit(name, col0, k0)
                loaded.append((w, wt, kc, k0))
                tot += kc
            i = 0
            for (w, wt, kc, k0) in loaded:
                for kk in range(kc):
                    mm(ps, w[:, kk * 128:(kk + 1) * 128], rhs_of_k(k0 + kk), i == 0, i == tot - 1,
                       [wt, rt_of_k(k0 + kk)] + list(extra_reads), [pt])
                    i += 1

        def hTk(k):
            return hT[:, k * NT:(k + 1) * NT]

        def phase_load(ti):
            xin = [arena_view(i * 16384, 4096, F32) for i in range(2)]
            t_xin = [Tile("xin0"), Tile("xin1")]
            stg = [arena_view(32768 + i * 2048, 512, F32) for i in range(4)]
            t_stg = [Tile("stg%d" % i) for i in range(4)]
            n = 0
            for tc in range(4):
                b = tc % 2
                dma("sp", xin[b], x_d[ti * NT + tc * 128: ti * NT + (tc + 1) * 128, :], [], [t_xin[b]])
                for c4 in range(8):
                    pb = 4 + (n % 2)
                    for q in range(4):
                        c = c4 * 4 + q
                        tr(psb[pb][:, q * 128:(q + 1) * 128], xin[b][:, c * 128:(c + 1) * 128], identf[:, :],
                           [t_xin[b], t_const], [t_ps[pb]])
                    s = n % 4
                    cp(stg[s], psb[pb][:, :], [t_ps[pb]], [t_stg[s]], eng=("act" if n % 2 else "dve"))
                    dma("sp", xT_d[c4 * 4:(c4 + 1) * 4, :, tc * 128:(tc + 1) * 128].rearrange("c p t -> p c t"),
                        stg[s].rearrange("p (c t) -> p c t", c=4), [t_stg[s]], t_xT[c4 * 4:(c4 + 1) * 4])
                    n += 1

        def norm_stats(nelem_inv, src_load, nchunks, ps_i, tag):
            raise NotImplementedError

        XC_OFF = 119808
        xc = [arena_view(XC_OFF + i * 2048, NT, F32) for i in range(3)]
        t_xc = [Tile("xc%d" % i) for i in range(3)]
        sq = [arena_view(XC_OFF + 6144 + i * 1024, NT, BF16) for i in range(2)]
        t_sq = [Tile("sq0"), Tile("sq1")]
        rstd = arena_view(XC_OFF + 8192, NT, F32)
        t_rstd = Tile("rstd")

        cnt = {"xc": 0, "sq": 0}

        def load_chunk(src_ap, src_tile):
            i = cnt["xc"] % 3
            cnt["xc"] += 1
            dma("sp", xc[i], src_ap, [src_tile], [t_xc[i]])
            return xc[i], t_xc[i]

        def phase_norm(gcol, final=False, ti_out=None):
            pss, tss = psb[6], t_ps[6]
            for c in range(32):
                a, t = load_chunk(xT_d[c], t_xT[c])
                s = cnt["sq"] % 2
                cnt["sq"] += 1
                act(sq[s], a, AF.Square, [t], [t_sq[s]])
                mm(pss[:, :], onesb[:, :], sq[s], c == 0, c == 31, [t_sq[s], t_const], [tss])
            ts(rstd, pss[:, :], 1.0 / D, EPS, ALU.mult, ALU.add, [tss], [t_rstd])
            act(rstd, rstd, AF.Sqrt, [t_rstd], [t_rstd])
            P.add("dve", lambda e: e.reciprocal(out=rstd, in_=rstd), reads=[t_rstd], writes=[t_rstd])
            if not final:
                for c in range(32):
                    a, t = load_chunk(xT_d[c], t_xT[c])
                    stt(hTk(c), a, cv[:, gcol + c:gcol + c + 1], rstd, ALU.mult, ALU.mult, [t, t_rstd, t_cv], [t_hT[c]])
                return
            otm = [arena_view(tc * 16384, 4096, F32) for tc in range(4)]
            t_otm = [Tile("otm%d" % i) for i in range(4)]
            on = [arena_view(65536 + i * 2048, NT, F32) for i in range(2)]
            t_on = [Tile("on0"), Tile("on1")]
            for c in range(32):
                a, t = load_chunk(xT_d[c], t_xT[c])
                s = c % 2
                stt(on[s], a, cv[:, gcol + c:gcol + c + 1], rstd, ALU.mult, ALU.mult, [t, t_rstd, t_cv], [t_on[s]])
                pb = 4 + (c % 2)
                if upto == 98:
                    continue
                for tc in range(4):
                    tr(psb[pb][:, tc * 128:(tc + 1) * 128], on[s][:, tc * 128:(tc + 1) * 128], identf[:, :],
                       [t_on[s], t_const], [t_ps[pb]])
                if upto == 97:
                    continue
                for tc in range(4):
                    cp(otm[tc][:, c * 128:(c + 1) * 128], psb[pb][:, tc * 128:(tc + 1) * 128], [t_ps[pb]], [t_otm[tc]],
                       eng=("act" if tc % 2 else "dve"))
            for tc in range(4):
                to = Tile("out")
                t_out.append(to)
                dma("sp", out_d[ti_out * NT + tc * 128: ti_out * NT + (tc + 1) * 128, :], otm[tc], [t_otm[tc]], [to])

        def phase_ffn(i):
            actT = ar
            t_act = [Tile("act%d" % m) for m in range(NFF)]
            sg = [arena_view(88 * 1024 + j * 2048, NT, F32) for j in range(2)]
            t_sg = [Tile("sg0"), Tile("sg1")]
            xo = [arena_view(92 * 1024 + j * 2048, NT, F32) for j in range(2)]
            t_xo = [Tile("xo0"), Tile("xo1")]
            for m in range(NFF):
                pg, pu = (0, 1) if m % 2 == 0 else (2, 3)
                gemm(psb[pg][:, :], t_ps[pg], [("g%d" % i, m * 128, 0)], hTk, lambda k: t_hT[k])
                gemm(psb[pu][:, :], t_ps[pu], [("u%d" % i, m * 128, 0)], hTk, lambda k: t_hT[k])
                s = m % 2
                act(sg[s], psb[pg][:, :], AF.Silu, [t_ps[pg]], [t_sg[s]])
                tt(actT[:, m * NT:(m + 1) * NT], sg[s], psb[pu][:, :], ALU.mult, [t_sg[s], t_ps[pu]], [t_act[m]])
            for m in range(32):
                pd = m % 4
                gemm(psb[pd][:, :], t_ps[pd], [("d%d" % i, m * 128, 0), ("d%d" % i, m * 128, 32), ("d%d" % i, m * 128, 64)],
                     lambda k: actT[:, k * NT:(k + 1) * NT], lambda k: t_act[k])
                a, t = load_chunk(xT_d[m], t_xT[m])
                s = m % 2
                stt(xo[s], psb[pd][:, :], 0.5, a, ALU.mult, ALU.add, [t_ps[pd], t], [t_xo[s]])
                dma("sp", xT_d[m], xo[s], [t_xo[s]], [t_xT[m]])

        def phase_convbranch(full):
            vT = ar
            t_v = [Tile("v%d" % c) for c in range(32)]
            ccs = [arena_view(32768 + j * 2048, NT, F32) for j in range(2)]
            t_ccs = [Tile("ccs0"), Tile("ccs1")]
            ue = [arena_view(36864 + j * 2304, NT + 2, F32) for j in range(2)]
            t_ue = [Tile("ue0"), Tile("ue1")]
            acc = [arena_view(41984 + j * 2048, NT, F32) for j in range(2)]
            t_acc = [Tile("acc0"), Tile("acc1")]
            for c in range(32):
                pc, px, pbk = (0, 1, 2) if c % 2 == 0 else (3, 4, 5)
                s = c % 2
                gemm(psb[pc][:, :], t_ps[pc], [("win", O_CC + c * 128, 0)], hTk, lambda k: t_hT[k])
                gemm(psb[px][:, :], t_ps[px], [("win", O_CX + c * 128, 0)], hTk, lambda k: t_hT[k])
                cp(ccs[s], psb[pc][:, :], [t_ps[pc]], [t_ccs[s]], eng="act")
                cp(ue[s][:, 0:2], uh[:, c * 2:c * 2 + 2], [t_uh[c]], [t_ue[s]])
                tt(ue[s][:, 2:NT + 2], ccs[s], psb[px][:, :], ALU.mult, [t_ccs[s], t_ps[px], t_ue[s]], [t_ue[s]])
                cp(uh[:, c * 2:c * 2 + 2], ue[s][:, NT:NT + 2], [t_ue[s]], [t_uh[c]])
                if not full:
                    continue
                w0 = cv[:, CV_SCW + c:CV_SCW + c + 1]
                w1 = cv[:, CV_SCW + 32 + c:CV_SCW + 32 + c + 1]
                w2 = cv[:, CV_SCW + 64 + c:CV_SCW + 64 + c + 1]
                ts1(acc[s], ue[s][:, 2:NT + 2], w2, ALU.mult, [t_ue[s], t_cv], [t_acc[s]])
                stt(acc[s], ue[s][:, 1:NT + 1], w1, acc[s], ALU.mult, ALU.add, [t_ue[s], t_cv, t_acc[s]], [t_acc[s]])
                stt(acc[s], ue[s][:, 0:NT], w0, acc[s], ALU.mult, ALU.add, [t_ue[s], t_cv, t_acc[s]], [t_acc[s]])
                gemm(psb[pbk][:, :], t_ps[pbk], [("win", O_CB + c * 128, 0)], hTk, lambda k: t_hT[k])
                tt(vT[:, c * NT:(c + 1) * NT], acc[s], psb[pbk][:, :], ALU.mult, [t_acc[s], t_ps[pbk]], [t_v[c]])
            if not full:
                return
            sig = [arena_view(46080 + j * 2048, NT, F32) for j in range(2)]
            t_sig = [Tile("sig0"), Tile("sig1")]
            mao = [arena_view(50176 + j * 2048, NT, F32) for j in range(2)]
            t_mao = [Tile("mao0"), Tile("mao1")]
            for c in range(32):
                pg, py = (0, 1) if c % 2 == 0 else (2, 3)
                s = c % 2
                gemm(psb[pg][:, :], t_ps[pg], [("win", O_GA + c * 128, 0)], hTk, lambda k: t_hT[k])
                act(sig[s], psb[pg][:, :], AF.Sigmoid, [t_ps[pg], t_cv], [t_sig[s]], bias=cv[:, CV_GB + c:CV_GB + c + 1])
                gemm(psb[py][:, :], t_ps[py], [("sco", c * 128, 0)], lambda k: vT[:, k * NT:(k + 1) * NT], lambda k: t_v[k])
                tt(mao[s], sig[s], psb[py][:, :], ALU.mult, [t_sig[s], t_ps[py]], [t_mao[s]])
                dma("sp", mA_d[c], mao[s], [t_mao[s]], [t_mA[c]])


        class Alloc:
            def __init__(self):
                self.o = 0

            def __call__(self, nbytes, nelem, dt):
                o = self.o
                self.o += (nbytes + 63) // 64 * 64
                assert self.o <= XC_OFF, self.o
                return arena_view(o, nelem, dt)

        def bc(ap_sb, off, F, dims):
            raise NotImplementedError

        def phase_ssd(ti, full, first):
            al = Alloc()
            xs_tm = al(32768, 4 * 4096, BF16)
            BT = al(8192, 8 * NT, BF16)
            CT = al(8192, 8 * NT, BF16)
            B_tm = al(8192, 4 * 1024, BF16)
            dt = al(2048, 512, F32)
            der = [[al(512, 128, F32) for j in range(6)] for tc in range(4)]
            xe = [al(2304, NT + 3, F32) for j in range(2)]
            cacc = [al(2048, NT, F32) for j in range(2)]
            xsT = [al(1024, NT, BF16) for j in range(2)]
            Sg = al(4096, 1024, F32)
            Sbf = al(2048, 1024, BF16)
            cbT = al(512, 128, F32)
            Ls = [al(2048, 512, F32) for j in range(2)]
            scT = [al(1024, 512, BF16) for j in range(2)]
            xdt = al(2048, 1024, BF16)
            xdd = al(2048, 1024, BF16)
            yg = al(4096, 1024, F32)
            Dx = al(2048, 512, F32)
            t1 = al(2048, 512, F32)
            stg = [al(2048, 512, F32) for j in range(2)]
            tmp = [al(512, 128, F32) for j in range(4)]
            t_xs = [Tile("xs_tm%d" % c) for c in range(32)]
            t_BT = [Tile("BT%d" % g) for g in range(8)]
            t_CT = [Tile("CT%d" % g) for g in range(8)]
            t_Btm = [Tile("Btm%d" % g) for g in range(8)]
            t_dt = [Tile("dt%d" % tc) for tc in range(4)]
            t_der = [Tile("der%d" % tc) for tc in range(4)]
            t_xe = [Tile("xe0"), Tile("xe1")]
            t_cacc = [Tile("cacc0"), Tile("cacc1")]
            t_xsT = [Tile("xsT0"), Tile("xsT1")]
            t_Sg, t_Sbf, t_cbT = Tile("Sg"), Tile("Sbf"), Tile("cbT")
            t_Ls = [Tile("Ls0"), Tile("Ls1")]
            t_scT = [Tile("scT0"), Tile("scT1")]
            t_xdt, t_xdd, t_yg, t_Dx, t_t1 = Tile("xdt"), Tile("xdd"), Tile("yg"), Tile("Dx"), Tile("t1")
            t_stg = [Tile("stg0"), Tile("stg1")]
            t_tmp = Tile("tmp")
            cn = {"n": 0}

            def conv_chunk(c):
                n = cn["n"]
                cn["n"] += 1
                pp = n % 4
                s = n % 2
                gemm(psb[pp][:, :], t_ps[pp], [("win", O_XBC + c * 128, 0)], hTk, lambda k: t_hT[k])
                cp(xe[s][:, 0:3], xh[:, c * 3:c * 3 + 3], [t_xh[c]], [t_xe[s]])
                cp(xe[s][:, 3:NT + 3], psb[pp][:, :], [t_ps[pp], t_xe[s]], [t_xe[s]], eng="act")
                cp(xh[:, c * 3:c * 3 + 3], xe[s][:, NT:NT + 3], [t_xe[s]], [t_xh[c]])
                wk = [cv[:, CV_CW + k * 80 + c:CV_CW + k * 80 + c + 1] for k in range(4)]
                ts(cacc[s], xe[s][:, 3:NT + 3], wk[3], cv[:, CV_CB + c:CV_CB + c + 1], ALU.mult, ALU.add,
                   [t_xe[s], t_cv], [t_cacc[s]])
                for k in (2, 1, 0):
                    stt(cacc[s], xe[s][:, k:NT + k], wk[k], cacc[s], ALU.mult, ALU.add, [t_xe[s], t_cv, t_cacc[s]], [t_cacc[s]])
                return s

            def r4(a):
                return a.rearrange("p (t c) -> p t c", t=4)

            for c in range(64, 80):
                s = conv_chunk(c)
                if c < 72:
                    g = c - 64
                    act(BT[:, g * NT:(g + 1) * NT], cacc[s], AF.Silu, [t_cacc[s]], [t_BT[g]])
                    hb = c % 2
                    for tc in range(4):
                        tr(pstb[:, hb * 512 + tc * 128: hb * 512 + (tc + 1) * 128],
                           BT[:, g * NT + tc * 128: g * NT + (tc + 1) * 128], identb[:, :], [t_BT[g], t_const], [t_pstb[hb]])
                    cp(r4(B_tm)[:, :, g * 128:(g + 1) * 128], r4(pstb[:, hb * 512:(hb + 1) * 512]), [t_pstb[hb]], [t_Btm[g]])
                else:
                    g = c - 72
                    act(CT[:, g * NT:(g + 1) * NT], cacc[s], AF.Silu, [t_cacc[s]], [t_CT[g]])
            w, wt, kc = next_unit("win", O_DT, 0)
            for tc in range(4):
                pp = 4 + tc % 2
                for k in range(32):
                    mm(psb[pp][:, 0:128], hT[:, k * NT + tc * 128: k * NT + (tc + 1) * 128], w[:, k * 128:(k + 1) * 128],
                       k == 0, k == 31, [wt, t_hT[k]], [t_ps[pp]])
                d_ = dt[:, tc * 128:(tc + 1) * 128]
                rd, wr = [t_tmp], [t_tmp]
                tt(tmp[0], psb[pp][:, 0:128], dtb_b, ALU.add, [t_ps[pp], t_rb] + rd, wr)
                ts1(tmp[1], tmp[0], -1.0, ALU.mult, rd, wr)
                tt(tmp[1], tmp[1], tmp[0], ALU.min, rd, wr)
                act(tmp[2], tmp[1], AF.Exp, rd, wr)
                act(tmp[2], tmp[2], AF.Ln, rd, wr, bias=1.0)
                ts1(tmp[3], tmp[0], 0.0, ALU.max, rd, wr)
                tt(d_, tmp[3], tmp[2], ALU.add, rd, [t_tmp, t_dt[tc]])
                a_, acs, ea, dte, nacs, cdb = der[tc]
                dd = [t_der[tc]]
                tt(a_, d_, Ab[:, :], ALU.mult, [t_dt[tc], t_const], dd)
                p1, p2 = 0, 1
                mm(psb[p1][:, 0:128], tri[:, :], a_, True, True, dd + [t_const], [t_ps[p1]])
                mm(psb[p2][:, 0:128], onesf[:, :], a_, True, True, dd + [t_const], [t_ps[p2]])
                cp(acs, psb[p1][:, 0:128], [t_ps[p1]] + dd, dd, eng="act")
                act(ea, acs, AF.Exp, dd, dd)
                tt(dte, psb[p2][:, 0:128], acs, ALU.subtract, [t_ps[p2]] + dd, dd)
                act(dte, dte, AF.Exp, dd, dd)
                act(cdb, psb[p2][:, 0:128], AF.Exp, [t_ps[p2]] + dd, dd)
                ts1(nacs, acs, -1.0, ALU.mult, dd, dd)

            def b3(ap2d, n1, n2):
                a = ap2d
                return bass.AP(a.tensor, a.offset, [list(a.ap[0]), [a.ap[-1][0], n1], [0, n2]])

            def v3(ap2d, n1, n2):
                return ap2d.rearrange("p (a b) -> p a b", a=n1)

            def cb3(ap2d, n1):
                a = ap2d
                return bass.AP(a.tensor, a.offset, [list(a.ap[0]), [0, n1], list(a.ap[-1])])

            for half in range(2):
                for cl in range(32):
                    c = half * 32 + cl
                    s = conv_chunk(c)
                    act(xsT[s], cacc[s], AF.Silu, [t_cacc[s]], [t_xsT[s]])
                    hb = c % 2
                    for tc in range(4):
                        tr(pstb[:, hb * 512 + tc * 128: hb * 512 + (tc + 1) * 128], xsT[s][:, tc * 128:(tc + 1) * 128],
                           identb[:, :], [t_xsT[s], t_const], [t_pstb[hb]])
                    cp(r4(xs_tm)[:, :, cl * 128:(cl + 1) * 128], r4(pstb[:, hb * 512:(hb + 1) * 512]), [t_pstb[hb]], [t_xs[cl]])
                for gl in range(4):
                    g = half * 4 + gl
                    if first:
                        P.add("dve", lambda e: e.memset(Sg, 0.0), writes=[t_Sg])
                    else:
                        dma("sp", Sg, S_d[g], [t_S[g]], [t_Sg])
                    xs_tiles = t_xs[gl * 8:(gl + 1) * 8]
                    for tc in range(4):
                        a_, acs, ea, dte, nacs, cdb = der[tc]
                        dd = [t_der[tc]]
                        tok = slice(g * NT + tc * 128, g * NT + (tc + 1) * 128)
                        mm(psb[0][:, 0:128], BT[:, tok], CT[:, tok], True, True, [t_BT[g], t_CT[g]], [t_ps[0]])
                        cp(cbT, psb[0][:, 0:128], [t_ps[0]], [t_cbT], eng="act")
                        cp(Sbf, Sg, [t_Sg], [t_Sbf])
                        for h2 in range(2):
                            mm(psb[1 + h2][:, :], CT[:, tok], Sbf[:, h2 * 512:(h2 + 1) * 512], True, True,
                               [t_CT[g], t_Sbf], [t_ps[1 + h2]])
                        xs_g = xs_tm[:, tc * 4096 + gl * 1024: tc * 4096 + (gl + 1) * 1024]
                        tt(v3(xdt, 16, 64), v3(xs_g, 16, 64), b3(dt[:, tc * 128 + g * 16: tc * 128 + (g + 1) * 16], 16, 64),
                           ALU.mult, xs_tiles + [t_dt[tc]], [t_xdt])
                        tt(v3(xdd, 16, 64), v3(xdt, 16, 64), b3(dte[:, g * 16:(g + 1) * 16], 16, 64), ALU.mult,
                           [t_xdt] + dd, [t_xdd])
                        for q4 in range(4):
                            s = q4 % 2
                            pl = 3 + s
                            for q in range(4):
                                h = g * 16 + q4 * 4 + q
                                mm(psb[pl][:, q * 128:(q + 1) * 128], a_[:, h:h + 1].to_broadcast([128, 128]), tri[:, :],
                                   True, False, dd + [t_const], [t_ps[pl]])
                                mm(psb[pl][:, q * 128:(q + 1) * 128], identf[:, :], maskf[:, :], False, True, [t_const], [t_ps[pl]])
                            for q in range(4):
                                h = g * 16 + q4 * 4 + q
                                act(Ls[s][:, q * 128:(q + 1) * 128], psb[pl][:, q * 128:(q + 1) * 128], AF.Exp,
                                    [t_ps[pl]] + dd, [t_Ls[s]], bias=nacs[:, h:h + 1])
                            tt(v3(scT[s], 4, 128), v3(Ls[s], 4, 128), cb3(cbT, 4), ALU.mult, [t_Ls[s], t_cbT], [t_scT[s]])
                            for q in range(4):
                                hl = q4 * 4 + q
                                py = 5 + hl // 8
                                mm(psb[py][:, (hl % 8) * 64:(hl % 8 + 1) * 64], scT[s][:, q * 128:(q + 1) * 128],
                                   xdt[:, hl * 64:(hl + 1) * 64], True, True, [t_scT[s], t_xdt], [t_ps[py]])
                        for h2 in range(2):
                            h0 = g * 16 + h2 * 8
                            tt(v3(t1, 8, 64), v3(psb[1 + h2][:, :], 8, 64), b3(ea[:, h0:h0 + 8], 8, 64), ALU.mult,
                               [t_ps[1 + h2]] + dd, [t_t1])
                            tt(v3(Dx, 8, 64), v3(xs_g[:, h2 * 512:(h2 + 1) * 512], 8, 64), b3(D_b[:, h0:h0 + 8], 8, 64), ALU.mult,
                               xs_tiles + [t_rb], [t_Dx])
                            tt(t1, t1, Dx, ALU.add, [t_t1, t_Dx], [t_t1])
                            tt(yg[:, h2 * 512:(h2 + 1) * 512], psb[5 + h2][:, :], t1, ALU.add, [t_ps[5 + h2], t_t1], [t_yg])
                        for h2 in range(2):
                            mm(psb[1 + h2][:, :], B_tm[:, tc * 1024 + g * 128: tc * 1024 + (g + 1) * 128],
                               xdd[:, h2 * 512:(h2 + 1) * 512], True, True, [t_Btm[g], t_xdd], [t_ps[1 + h2]])
                        tt(v3(Sg, 16, 64), v3(Sg, 16, 64), b3(cdb[:, g * 16:(g + 1) * 16], 16, 64), ALU.mult, [t_Sg] + dd, [t_Sg])
                        for h2 in range(2):
                            tt(Sg[:, h2 * 512:(h2 + 1) * 512], Sg[:, h2 * 512:(h2 + 1) * 512], psb[1 + h2][:, :], ALU.add,
                               [t_Sg, t_ps[1 + h2]], [t_Sg])
                        if full:
                            for c4 in range(2):
                                s = c4
                                for q in range(4):
                                    cc = c4 * 4 + q
                                    tr(psb[0][:, q * 128:(q + 1) * 128], yg[:, cc * 128:(cc + 1) * 128], identf[:, :],
                                       [t_yg, t_const], [t_ps[0]])
                                cp(stg[s], psb[0][:, :], [t_ps[0]], [t_stg[s]], eng="act")
                                c0 = g * 8 + c4 * 4
                                dma("sp", yT_d[c0:c0 + 4, :, tc * 128:(tc + 1) * 128].rearrange("c p t -> p c t"),
                                    r4(stg[s]), [t_stg[s]], t_yT[c0:c0 + 4])
                    dma("sp", S_d[g], Sg, [t_Sg], [t_S[g]])

        def phase_post():
            al = Alloc()
            ynT = al(65536, 64 * NT, BF16)
            mT = al(32768, 32 * NT, BF16)
            yz = al(16384, 8 * NT, F32)
            sz = [al(2048, NT, F32) for j in range(2)]
            t_yn = [Tile("yn%d" % c) for c in range(64)]
            t_m = [Tile("m%d" % c) for c in range(32)]
            t_yz = [Tile("yz%d" % c) for c in range(8)]
            t_sz = [Tile("sz0"), Tile("sz1")]
            for g in range(8):
                for cc in range(8):
                    c = g * 8 + cc
                    pp = c % 4
                    s = c % 2
                    gemm(psb[pp][:, :], t_ps[pp], [("win", O_Z + c * 128, 0)], hTk, lambda k: t_hT[k])
                    act(sz[s], psb[pp][:, :], AF.Silu, [t_ps[pp]], [t_sz[s]])
                    a, t = load_chunk(yT_d[c], t_yT[c])
                    tt(yz[:, cc * NT:(cc + 1) * NT], sz[s], a, ALU.mult, [t_sz[s], t], [t_yz[cc]])
                    s2 = cnt["sq"] % 2
                    cnt["sq"] += 1
                    act(sq[s2], yz[:, cc * NT:(cc + 1) * NT], AF.Square, [t_yz[cc]], [t_sq[s2]])
                    mm(psb[6][:, :], onesb[:, :], sq[s2], cc == 0, cc == 7, [t_sq[s2], t_const], [t_ps[6]])
                ts(rstd, psb[6][:, :], 1.0 / 1024, EPS, ALU.mult, ALU.add, [t_ps[6]], [t_rstd])
                act(rstd, rstd, AF.Sqrt, [t_rstd], [t_rstd])
                P.add("dve", lambda e: e.reciprocal(out=rstd, in_=rstd), reads=[t_rstd], writes=[t_rstd])
                for cc in range(8):
                    c = g * 8 + cc
                    stt(ynT[:, c * NT:(c + 1) * NT], yz[:, cc * NT:(cc + 1) * NT], cv[:, CV_SN + c:CV_SN + c + 1], rstd,
                        ALU.mult, ALU.mult, [t_yz[cc], t_rstd, t_cv], [t_yn[c]])
            sig = [yz[:, j * NT:(j + 1) * NT] for j in range(2)]
            tmpm = [yz[:, (2 + j) * NT:(3 + j) * NT] for j in range(2)]
            t_sig = [t_yz[0], t_yz[1]]
            t_tm = [t_yz[2], t_yz[3]]
            for c in range(32):
                pg, py = (0, 1) if c % 2 == 0 else (2, 3)
                s = c % 2
                gemm(psb[pg][:, :], t_ps[pg], [("win", O_GB + c * 128, 0)], hTk, lambda k: t_hT[k])
                act(sig[s], psb[pg][:, :], AF.Sigmoid, [t_ps[pg], t_cv], [t_sig[s]], bias=cv[:, CV_GB + 32 + c:CV_GB + 32 + c + 1])
                gemm(psb[py][:, :], t_ps[py], [("sso", c * 128, 0), ("sso", c * 128, 32)],
                     lambda k: ynT[:, k * NT:(k + 1) * NT], lambda k: t_yn[k])
                a, t = load_chunk(mA_d[c], t_mA[c])
                tt(tmpm[s], sig[s], psb[py][:, :], ALU.mult, [t_sig[s], t_ps[py]], [t_tm[s]])
                tt(mT[:, c * NT:(c + 1) * NT], tmpm[s], a, ALU.add, [t_tm[s], t], [t_m[c]])
            xo = [sz[0], sz[1]]
            t_xo = t_sz
            for c in range(32):
                pp = 4 + c % 2
                gemm(psb[pp][:, :], t_ps[pp], [("wo", c * 128, 0)], lambda k: mT[:, k * NT:(k + 1) * NT], lambda k: t_m[k])
                a, t = load_chunk(xT_d[c], t_xT[c])
                s = c % 2
                tt(xo[s], psb[pp][:, :], a, ALU.add, [t_ps[pp], t], [t_xo[s]])
                dma("sp", xT_d[c], xo[s], [t_xo[s]], [t_xT[c]])

        def dbg_dump(i):
            if debug:
                P.barrier()
                td = Tile("dbg%d" % i)
                t_out.append(td)
                dma("sp", dbg_d[i], xT_d, t_xT, [td])
                P.barrier()

        nfull_seen = 0
        for ti, mode in enumerate(modes):
            full = mode == "full"
            wstate["cur"] = (ulists[mode], ubase[mode], 0)
            wstate["i"] = 0
            P.barrier()
            phase_load(ti)
            P.barrier()
            if upto <= 0:
                dbg_dump(0)
                break
            phase_norm(CV_N1)
            P.barrier()
            if upto <= 1:
                dbg_dump(0)
                break
            phase_ffn(1)
            if ti == 0:
                dbg_dump(0)
            if upto <= 2:
                break
            phase_norm(CV_NM)
            P.barrier()
            phase_convbranch(full)
            P.barrier()
            if upto <= 3:
                break
            phase_ssd(ti, full, ti == 0)
            if upto <= 4:
                P.barrier()
                td = Tile("dby")
                t_out.append(td)
                dma("sp", dby_d, yT_d, t_yT, [td])
                break
            if full:
                if debug and ti == 0:
                    P.barrier()
                    td = Tile("dby")
                    t_out.append(td)
                    dma("sp", dby_d, yT_d, t_yT, [td])
                P.barrier()
                phase_post()
                if ti == 0:
                    dbg_dump(1)
                if upto <= 5:
                    break
                phase_norm(CV_N2)
                P.barrier()
                phase_ffn(2)
                if ti == 0:
                    dbg_dump(2)
                if upto <= 6:
                    break
                P.barrier()
                phase_norm(CV_NF, final=True, ti_out=nfull_seen)
                nfull_seen += 1
            assert upto < 99 or wstate["i"] == len(ulists[mode]), (wstate["i"], len(ulists[mode]))
        P.emit(final_tiles=t_out)
    return nc, nu, ulists, ubase


def _colvec(v, nchunk):
    return np.ascontiguousarray(np.asarray(v, np.float32).reshape(nchunk, 128).T)


def make_cvec(inp):
    cvh = np.zeros((128, NCV), np.float32)
    cvh[:, CV_N1:CV_N1 + 32] = _colvec(inp["ffn1_norm"][0], 32)
    cvh[:, CV_NM:CV_NM + 32] = _colvec(inp["mix_norm"][0], 32)
    cvh[:, CV_N2:CV_N2 + 32] = _colvec(inp["ffn2_norm"][0], 32)
    cvh[:, CV_NF:CV_NF + 32] = _colvec(inp["final_norm"], 32)
    cvh[:, CV_GB:CV_GB + 64] = _colvec(inp["gate_bias"][0], 64)
    for k in range(3):
        cvh[:, CV_SCW + k * 32:CV_SCW + (k + 1) * 32] = _colvec(inp["sconv_w"][0, k], 32)
    for k in range(4):
        cvh[:, CV_CW + k * 80:CV_CW + (k + 1) * 80] = _colvec(inp["ssm_conv_w"][0, k], 80)
    cvh[:, CV_CB:CV_CB + 80] = _colvec(inp["ssm_conv_b"][0], 80)
    cvh[:, CV_SN:CV_SN + 64] = _colvec(inp["ssm_norm"][0], 64)
    rbh = np.zeros((128, 384), np.float32)
    rbh[:, 0:128] = np.asarray(inp["ssm_dt_bias"][0], np.float32)[None, :]
    rbh[:, 128:256] = np.asarray(inp["ssm_A_log"][0], np.float32)[None, :]
    rbh[:, 256:384] = np.asarray(inp["ssm_D"][0], np.float32)[None, :]
    return cvh, rbh


def make_wstream(inp, nu, ulists, ubase):
    W = {"g1": inp["ffn1_w_gate"][0], "u1": inp["ffn1_w_up"][0], "d1": inp["ffn1_w_down"][0],
         "g2": inp["ffn2_w_gate"][0], "u2": inp["ffn2_w_up"][0], "d2": inp["ffn2_w_down"][0],
         "win": inp["w_in"][0], "sco": inp["sconv_w_out"][0], "sso": inp["ssm_w_out"][0], "wo": inp["w_o"][0]}
    W = {k: np.asarray(v, np.float32) for k, v in W.items()}
    ws = np.zeros((nu, 128, 4096), np.float32)
    for m, ul in ulists.items():
        b = ubase[m]
        for i, (name, col0, k0, kc) in enumerate(ul):
            blk = W[name][k0 * 128:(k0 + kc) * 128, col0:col0 + 128]
            ws[b + i, :, :kc * 128] = blk.reshape(kc, 128, 128).transpose(1, 0, 2).reshape(128, kc * 128)
    return ws


_CACHE = {}


def kernel(**inp):
    x = np.asarray(inp["x"], np.float32)
    modes = ["full"] * 4
    key = "full4"
    if key not in _CACHE:
        _CACHE[key] = build(modes)
    nc, nu, ulists, ubase = _CACHE[key]
    cvh, rbh = make_cvec(inp)
    ws = make_wstream(inp, nu, ulists, ubase)
    smh = np.ones((128, 1), np.float32)
    in_maps = []
    for core in range(8):
        b = core % 4
        in_maps.append({"x": np.ascontiguousarray(x[b]), "ws": ws, "cv": cvh, "rb": rbh, "sm": smh})
    res = run_bass_kernel_spmd(nc, in_maps, core_ids=list(range(8)))
    out = np.stack([np.asarray(res.results[b]["out"], np.float32) for b in range(4)], axis=0)
    return out
```

```python
from contextlib import ExitStack
import numpy as np
import concourse.bass as bass
import concourse.mybir as mybir
from concourse.bass_utils import run_bass_kernel_spmd

F32 = mybir.dt.float32
BF16 = mybir.dt.bfloat16
AF = mybir.ActivationFunctionType
ALU = mybir.AluOpType

D = 4096
DFF = 11008
NFF = 86
SEQ = 2048
NT = 512
EPS = 1e-5
O_CB, O_CC, O_CX, O_Z, O_XBC, O_DT, O_GA, O_GB = 0, 4096, 8192, 12288, 20480, 30720, 30848, 34944
CV_N1, CV_NM, CV_N2, CV_NF, CV_GB, CV_SCW, CV_CW, CV_CB, CV_SN, NCV = 0, 32, 64, 96, 128, 192, 288, 608, 688, 752

ENGS = ("pe", "act", "dve", "pool", "sp")
RING = {"pool": 8, "sp": 16, "act": 8}


class Tile:
    __slots__ = ("name", "w", "wd", "r", "rd", "x", "xl")

    def __init__(self, name, excl=False):
        self.name = name
        self.x = excl
        self.xl = {}
        self.w = {}
        self.wd = []
        self.r = {}
        self.rd = []


class Op:
    __slots__ = ("eng", "fn", "is_dma", "deps", "signal", "tok_val", "sem", "idx")


class Prog:
    def __init__(self, nc, stack):
        self.nc = nc
        self.ops = {e: [] for e in ENGS}
        self.n = 0
        self.esem = {e: stack.enter_context(nc.semaphore("es_" + e)) for e in ENGS}
        self.ring = {}
        for q, n in RING.items():
            self.ring[q] = [[stack.enter_context(nc.semaphore("rs_%s%d" % (q, i))), 0, None] for i in range(n)]
        self.rpos = {q: 0 for q in RING}
        self.bar = []
        self.bar_seen = set()

    def barrier(self):
        last = []
        for e in ("pe", "act", "dve"):
            for op in reversed(self.ops[e]):
                if (not op.is_dma) and op.fn is not None:
                    last.append(op)
                    break
        for q in ("sp", "act"):
            for slot in self.ring[q]:
                if slot[2] is not None:
                    last.append(slot[2])
        self.bar = last
        self.bar_seen = set()

    def add(self, eng, fn, reads=(), writes=(), dma=False):
        op = Op()
        op.eng, op.fn, op.is_dma, op.signal, op.tok_val, op.sem = eng, fn, dma, False, None, None
        op.idx = self.n
        self.n += 1
        deps = {}
        for t in reads:
            for w in t.w.values():
                deps[w.idx] = w
            for w in t.wd:
                deps[w.idx] = w
        for t in writes:
            for w in t.w.values():
                deps[w.idx] = w
            for w in t.wd:
                deps[w.idx] = w
            for r in t.r.values():
                deps[r.idx] = r
            for r in t.rd:
                deps[r.idx] = r
        for t in list(reads) + list(writes):
            if t.x:
                for e2, o2 in t.xl.items():
                    if e2 != eng:
                        deps[o2.idx] = o2
                t.xl[eng] = op
        if eng != "pool" and self.bar and eng not in self.bar_seen:
            self.bar_seen.add(eng)
            for b in self.bar:
                deps[b.idx] = b
        if dma:
            slot = self.ring[eng][self.rpos[eng] % len(self.ring[eng])]
            self.rpos[eng] += 1
            if slot[2] is not None:
                deps[slot[2].idx] = slot[2]
            slot[1] += 16
            slot[2] = op
            op.sem, op.tok_val = slot[0], slot[1]
        for t in reads:
            if dma:
                t.rd.append(op)
            else:
                t.r[eng] = op
        for t in writes:
            t.w, t.wd, t.r, t.rd = {}, [], {}, []
            if dma:
                t.wd.append(op)
            else:
                t.w[eng] = op
        dl = []
        for d in deps.values():
            if (not d.is_dma) and (not dma) and d.eng == "pe" and eng == "pe":
                continue
            if not d.is_dma:
                d.signal = True
            dl.append(d)
        op.deps = dl
        self.ops[eng].append(op)
        return op

    def emit(self, final_tiles=()):
        nc = self.nc
        self.add("sp", None, reads=list(final_tiles))
        for e in ENGS:
            c = 0
            for op in self.ops[e]:
                if (not op.is_dma) and op.signal:
                    c += 1
                    op.tok_val = c
        prog = self

        def run(e, eng):
            waited = {}
            for op in prog.ops[e]:
                for d in op.deps:
                    sem, val = (d.sem, d.tok_val) if d.is_dma else (prog.esem[d.eng], d.tok_val)
                    k = id(sem)
                    if waited.get(k, 0) >= val:
                        continue
                    waited[k] = val
                    eng.wait_ge(sem, val)
                if op.fn is None:
                    continue
                inst = op.fn(eng)
                if op.is_dma:
                    inst.then_inc(op.sem, 16)
                elif op.signal:
                    inst.then_inc(prog.esem[e], 1)

        with nc.Block() as block:
            @block.tensor
            def _(eng):
                run("pe", eng)

            @block.scalar
            def _(eng):
                run("act", eng)

            @block.vector
            def _(eng):
                run("dve", eng)

            @block.gpsimd
            def _(eng):
                run("pool", eng)

            @block.sync
            def _(eng):
                run("sp", eng)


class SB:
    def __init__(self, h, F):
        self.h, self.F = h, F

    def __getitem__(self, key):
        return self.h[key]

    def v(self, off, dims, npart=128):
        return bass.AP(self.h, off, [[self.F, npart]] + [list(d) for d in dims])


def unit_list(mode="full"):
    u = []

    def ffn(i):
        for m in range(NFF):
            u.append(("g%d" % i, m * 128, 0, 32))
            u.append(("u%d" % i, m * 128, 0, 32))
        for m in range(32):
            u.append(("d%d" % i, m * 128, 0, 32))
            u.append(("d%d" % i, m * 128, 32, 32))
            u.append(("d%d" % i, m * 128, 64, 22))

    ffn(1)
    if mode == "full":
        for c in range(32):
            u.append(("win", O_CC + c * 128, 0, 32))
            u.append(("win", O_CX + c * 128, 0, 32))
            u.append(("win", O_CB + c * 128, 0, 32))
        for c in range(32):
            u.append(("win", O_GA + c * 128, 0, 32))
            u.append(("sco", c * 128, 0, 32))
    else:
        for c in range(32):
            u.append(("win", O_CC + c * 128, 0, 32))
            u.append(("win", O_CX + c * 128, 0, 32))
    for c in range(64, 80):
        u.append(("win", O_XBC + c * 128, 0, 32))
    u.append(("win", O_DT, 0, 32))
    for c in range(64):
        u.append(("win", O_XBC + c * 128, 0, 32))
    if mode == "full":
        for c in range(64):
            u.append(("win", O_Z + c * 128, 0, 32))
        for c in range(32):
            u.append(("win", O_GB + c * 128, 0, 32))
            u.append(("sso", c * 128, 0, 32))
            u.append(("sso", c * 128, 32, 32))
        for c in range(32):
            u.append(("wo", c * 128, 0, 32))
        ffn(2)
    return u


def build(modes, debug=False, upto=99):
    ntile = len(modes)
    nfull = sum(1 for m in modes if m == "full")
    ulists = {m: unit_list(m) for m in set(modes) | {"full"}}
    ubase = {m: 0 for m in ulists}
    nu = len(ulists["full"])
    uindex = {(n_, c_, k_): i for i, (n_, c_, k_, kc_) in enumerate(ulists["full"])}

    nc = bass.Bass("TRN2", target_bir_lowering=False)
    x_d = nc.dram_tensor("x", [ntile * NT, D], F32, kind="ExternalInput").ap()
    ws_d = nc.dram_tensor("ws", [nu, 128, 4096], F32, kind="ExternalInput").ap()
    cv_d = nc.dram_tensor("cv", [128, NCV], F32, kind="ExternalInput").ap()
    rb_d = nc.dram_tensor("rb", [128, 384], F32, kind="ExternalInput").ap()
    sm_d = nc.dram_tensor("sm", [128, 1], F32, kind="ExternalInput").ap()
    out_d = nc.dram_tensor("out", [nfull * NT, D], F32, kind="ExternalOutput").ap()
    xT_d = nc.dram_tensor("xT_s", [32, 128, NT], F32).ap()
    mA_d = nc.dram_tensor("mA_s", [32, 128, NT], F32).ap()
    yT_d = nc.dram_tensor("yT_s", [64, 128, NT], F32).ap()
    S_d = nc.dram_tensor("S_s", [8, 128, 1024], F32).ap()
    dbg_d = None
    if debug:
        dbg_d = nc.dram_tensor("dbg", [4, 32, 128, NT], F32, kind="ExternalOutput").ap()
        dby_d = nc.dram_tensor("dby", [64, 128, NT], F32, kind="ExternalOutput").ap()

    with ExitStack() as st:
        P = Prog(nc, st)

        def sb(name, F, dt):
            return SB(st.enter_context(nc.sbuf_tensor("s_" + name, [128, F], dt)), F)

        def pst(name, F, dt):
            return SB(st.enter_context(nc.psum_tensor("p_" + name, [128, F], dt)), F)

        cv = sb("cv", NCV, F32)
        rb = sb("rb", 384, F32)
        smk = sb("smk", 1, F32)
        identf = sb("identf", 128, F32)
        identb = sb("identb", 128, BF16)
        tri = sb("tri", 128, F32)
        maskf = sb("maskf", 128, F32)
        onesf = sb("onesf", 128, F32)
        onesb = sb("onesb", 128, BF16)
        Ab = sb("Ab", 128, F32)
        xh = sb("xh", 80 * 3, F32)
        uh = sb("uh", 32 * 2, F32)
        NSLOT = 5
        wsl = [sb("wsl%d" % i, 4096, BF16) for i in range(NSLOT)]
        hT = sb("hT", 32 * NT, BF16)
        ARENA = 65024
        ar = sb("arena", ARENA, BF16)
        arf = SB(ar.h, ARENA)

        def arena_view(off_bytes, nelem, dt):
            if dt == BF16:
                a = ar[:, off_bytes // 2: off_bytes // 2 + nelem]
            else:
                a = ar[:, off_bytes // 2: off_bytes // 2 + nelem * 2].bitcast(F32)
            return a

        psb = [pst("ps%d" % i, 512, F32) for i in range(7)]
        pstb = pst("pstb", 1024, BF16)
        t_ps = [Tile("ps%d" % i, True) for i in range(7)]
        _tp = Tile("pstb", True)
        t_pstb = [_tp, _tp]

        t_cv, t_rb, t_const = Tile("cv"), Tile("rb"), Tile("const")
        t_wsl = [Tile("wsl%d" % i) for i in range(NSLOT)]
        t_hT = [Tile("hT%d" % i) for i in range(32)]
        t_xT = [Tile("xT%d" % i) for i in range(32)]
        t_mA = [Tile("mA%d" % i) for i in range(32)]
        t_yT = [Tile("yT%d" % i) for i in range(64)]
        t_S = [Tile("S%d" % i) for i in range(8)]
        t_xh = [Tile("xh%d" % i) for i in range(80)]
        t_uh = [Tile("uh%d" % i) for i in range(32)]
        t_out = []

        def mm(ps, lhsT, rhs, start, stop, reads, writes):
            P.add("pe", lambda e: e.matmul(ps, lhsT=lhsT, rhs=rhs, start=start, stop=stop), reads=reads, writes=writes)

        def tr(ps, in_, ident, reads, writes):
            P.add("pe", lambda e: e.transpose(ps, in_, ident), reads=reads, writes=writes)

        def act(out, in_, func, reads, writes, bias=0.0, scale=1.0):
            P.add("act", lambda e: e.activation(out=out, in_=in_, func=func, bias=bias, scale=scale), reads=reads, writes=writes)

        def tt(out, in0, in1, op, reads, writes, eng="dve"):
            P.add(eng, lambda e: e.tensor_tensor(out=out, in0=in0, in1=in1, op=op), reads=reads, writes=writes)

        def ts(out, in0, s1, s2, op0, op1, reads, writes, eng="dve"):
            P.add(eng, lambda e: e.tensor_scalar(out=out, in0=in0, scalar1=s1, scalar2=s2, op0=op0, op1=op1), reads=reads, writes=writes)

        def ts1(out, in0, s1, op0, reads, writes, eng="dve"):
            P.add(eng, lambda e: e.tensor_single_scalar(out=out, in_=in0, scalar=s1, op=op0), reads=reads, writes=writes)

        def stt(out, in0, scalar, in1, op0, op1, reads, writes, eng="dve"):
            P.add(eng, lambda e: e.scalar_tensor_tensor(out=out, in0=in0, scalar=scalar, in1=in1, op0=op0, op1=op1), reads=reads, writes=writes)

        def cp(out, in_, reads, writes, eng="dve"):
            if eng == "act":
                P.add("act", lambda e: e.copy(out=out, in_=in_), reads=reads, writes=writes)
            else:
                P.add(eng, lambda e: e.tensor_copy(out=out, in_=in_), reads=reads, writes=writes)

        def dma(q, out, in_, reads, writes):
            P.add(q, lambda e: e.dma_start(out=out, in_=in_), reads=reads, writes=writes, dma=True)

        dma("sp", cv[:, :], cv_d, [], [t_cv])
        dma("sp", rb[:, :], rb_d, [], [t_rb])
        dma("sp", smk[:, :], sm_d, [], [t_rb])
        cst = [t_const]
        P.add("pool", lambda e: e.memset(identf[:, :], 1.0), writes=cst)
        P.add("pool", lambda e: e.affine_select(out=identf[:, :], in_=identf[:, :], pattern=[[1, 128]], compare_op=ALU.is_equal,
                                                fill=0.0, base=0, channel_multiplier=-1), reads=cst, writes=cst)
        P.add("pool", lambda e: e.tensor_copy(out=identb[:, :], in_=identf[:, :]), reads=cst, writes=cst)
        P.add("pool", lambda e: e.memset(tri[:, :], 1.0), reads=cst, writes=cst)
        P.add("pool", lambda e: e.affine_select(out=tri[:, :], in_=tri[:, :], pattern=[[1, 128]], compare_op=ALU.is_ge,
                                                fill=0.0, base=0, channel_multiplier=-1), reads=cst, writes=cst)
        P.add("pool", lambda e: e.memset(maskf[:, :], 0.0), reads=cst, writes=cst)
        P.add("pool", lambda e: e.affine_select(out=maskf[:, :], in_=maskf[:, :], pattern=[[1, 128]], compare_op=ALU.is_ge,
                                                fill=-30000.0, base=0, channel_multiplier=-1), reads=cst, writes=cst)
        P.add("pool", lambda e: e.memset(onesf[:, :], 1.0), reads=cst, writes=cst)
        P.add("pool", lambda e: e.memset(onesb[:, :], 1.0), reads=cst, writes=cst)
        P.add("pool", lambda e: e.memset(xh[:, :], 0.0), writes=t_xh)
        P.add("pool", lambda e: e.memset(uh[:, :], 0.0), writes=t_uh)
        act(Ab[:, :], rb[:, 128:256], AF.Exp, [t_rb], cst)
        ts1(Ab[:, :], Ab[:, :], -1.0, ALU.mult, cst, cst)
        dtb_b = rb[:, 0:128]
        D_b = rb[:, 256:384]
        CRD = [t_cv, t_rb, t_const]

        wstate = {"slot": 0, "cur": None, "i": 0}

        def next_unit(expect_name, expect_col, expect_k0):
            ul, base, i = wstate["cur"]
            name, col0, k0, kc = ul[wstate["i"]]
            assert (name, col0, k0) == (expect_name, expect_col, expect_k0), (name, col0, k0, expect_name, expect_col, expect_k0)
            s = wstate["slot"] % NSLOT
            wstate["slot"] += 1
            uidx = uindex[(name, col0, k0)]
            wstate["i"] += 1
            dma("pool", wsl[s][:, 0:kc * 128], ws_d[uidx][:, 0:kc * 128], [], [t_wsl[s]])
            return wsl[s], t_wsl[s], kc

        def gemm(ps, pt, units, rhs_of_k, rt_of_k, extra_reads=()):
            loaded = []
            tot = 0
            for (name, col0, k0) in units:
                w, wt, kc = next_unit(name, col0, k0)
                loaded.append((w, wt, kc, k0))
                tot += kc
            i = 0
            for (w, wt, kc, k0) in loaded:
                for kk in range(kc):
                    mm(ps, w[:, kk * 128:(kk + 1) * 128], rhs_of_k(k0 + kk), i == 0, i == tot - 1,
                       [wt, rt_of_k(k0 + kk)] + list(extra_reads), [pt])
                    i += 1

        def hTk(k):
            return hT[:, k * NT:(k + 1) * NT]

        def phase_load(ti):
            xin = [arena_view(i * 16384, 4096, F32) for i in range(2)]
            t_xin = [Tile("xin0"), Tile("xin1")]
            stg = [arena_view(32768 + i * 2048, 512, F32) for i in range(4)]
            t_stg = [Tile("stg%d" % i) for i in range(4)]
            n = 0
            for tc in range(4):
                b = tc % 2
                dma("sp", xin[b], x_d[ti * NT + tc * 128: ti * NT + (tc + 1) * 128, :], [], [t_xin[b]])
                for c4 in range(8):
                    pb = 4 + (n % 2)
                    for q in range(4):
                        c = c4 * 4 + q
                        tr(psb[pb][:, q * 128:(q + 1) * 128], xin[b][:, c * 128:(c + 1) * 128], identf[:, :],
                           [t_xin[b], t_const], [t_ps[pb]])
                    s = n % 4
                    cp(stg[s], psb[pb][:, :], [t_ps[pb]], [t_stg[s]], eng=("act" if n % 2 else "dve"))
                    dma("sp", xT_d[c4 * 4:(c4 + 1) * 4, :, tc * 128:(tc + 1) * 128].rearrange("c p t -> p c t"),
                        stg[s].rearrange("p (c t) -> p c t", c=4), [t_stg[s]], t_xT[c4 * 4:(c4 + 1) * 4])
                    n += 1

        def norm_stats(nelem_inv, src_load, nchunks, ps_i, tag):
            raise NotImplementedError

        XC_OFF = 119808
        xc = [arena_view(XC_OFF + i * 2048, NT, F32) for i in range(3)]
        t_xc = [Tile("xc%d" % i) for i in range(3)]
        sq = [arena_view(XC_OFF + 6144 + i * 1024, NT, BF16) for i in range(2)]
        t_sq = [Tile("sq0"), Tile("sq1")]
        rstd = arena_view(XC_OFF + 8192, NT, F32)
        t_rstd = Tile("rstd")

        cnt = {"xc": 0, "sq": 0}

        def load_chunk(src_ap, src_tile):
            i = cnt["xc"] % 3
            cnt["xc"] += 1
            dma("sp", xc[i], src_ap, [src_tile], [t_xc[i]])
            return xc[i], t_xc[i]

        def phase_norm(gcol, final=False, ti_out=None):
            pss, tss = psb[6], t_ps[6]
            for c in range(32):
                a, t = load_chunk(xT_d[c], t_xT[c])
                s = cnt["sq"] % 2
                cnt["sq"] += 1
                act(sq[s], a, AF.Square, [t], [t_sq[s]])
                mm(pss[:, :], onesb[:, :], sq[s], c == 0, c == 31, [t_sq[s], t_const], [tss])
            ts(rstd, pss[:, :], 1.0 / D, EPS, ALU.mult, ALU.add, [tss], [t_rstd])
            act(rstd, rstd, AF.Sqrt, [t_rstd], [t_rstd])
            P.add("dve", lambda e: e.reciprocal(out=rstd, in_=rstd), reads=[t_rstd], writes=[t_rstd])
            if not final:
                for c in range(32):
                    a, t = load_chunk(xT_d[c], t_xT[c])
                    stt(hTk(c), a, cv[:, gcol + c:gcol + c + 1], rstd, ALU.mult, ALU.mult, [t, t_rstd, t_cv], [t_hT[c]])
                return
            otm = [arena_view(tc * 16384, 4096, F32) for tc in range(4)]
            t_otm = [Tile("otm%d" % i) for i in range(4)]
            on = [arena_view(65536 + i * 2048, NT, F32) for i in range(2)]
            t_on = [Tile("on0"), Tile("on1")]
            for c in range(32):
                a, t = load_chunk(xT_d[c], t_xT[c])
                s = c % 2
                stt(on[s], a, cv[:, gcol + c:gcol + c + 1], rstd, ALU.mult, ALU.mult, [t, t_rstd, t_cv], [t_on[s]])
                pb = 4 + (c % 2)
                if upto == 98:
                    continue
                for tc in range(4):
                    tr(psb[pb][:, tc * 128:(tc + 1) * 128], on[s][:, tc * 128:(tc + 1) * 128], identf[:, :],
                       [t_on[s], t_const], [t_ps[pb]])
                if upto == 97:
                    continue
                for tc in range(4):
                    cp(otm[tc][:, c * 128:(c + 1) * 128], psb[pb][:, tc * 128:(tc + 1) * 128], [t_ps[pb]], [t_otm[tc]],
                       eng=("act" if tc % 2 else "dve"))
            for tc in range(4):
                to = Tile("out")
                t_out.append(to)
                dma("sp", out_d[ti_out * NT + tc * 128: ti_out * NT + (tc + 1) * 128, :], otm[tc], [t_otm[tc]], [to])

        def phase_ffn(i):
            actT = ar
            t_act = [Tile("act%d" % m) for m in range(NFF)]
            sg = [arena_view(88 * 1024 + j * 2048, NT, F32) for j in range(2)]
            t_sg = [Tile("sg0"), Tile("sg1")]
            xo = [arena_view(92 * 1024 + j * 2048, NT, F32) for j in range(2)]
            t_xo = [Tile("xo0"), Tile("xo1")]
            for m in range(NFF):
                pg, pu = (0, 1) if m % 2 == 0 else (2, 3)
                gemm(psb[pg][:, :], t_ps[pg], [("g%d" % i, m * 128, 0)], hTk, lambda k: t_hT[k])
                gemm(psb[pu][:, :], t_ps[pu], [("u%d" % i, m * 128, 0)], hTk, lambda k: t_hT[k])
                s = m % 2
                act(sg[s], psb[pg][:, :], AF.Silu, [t_ps[pg]], [t_sg[s]])
                tt(actT[:, m * NT:(m + 1) * NT], sg[s], psb[pu][:, :], ALU.mult, [t_sg[s], t_ps[pu]], [t_act[m]])
            for m in range(32):
                pd = m % 4
                gemm(psb[pd][:, :], t_ps[pd], [("d%d" % i, m * 128, 0), ("d%d" % i, m * 128, 32), ("d%d" % i, m * 128, 64)],
                     lambda k: actT[:, k * NT:(k + 1) * NT], lambda k: t_act[k])
                a, t = load_chunk(xT_d[m], t_xT[m])
                s = m % 2
                stt(xo[s], psb[pd][:, :], 0.5, a, ALU.mult, ALU.add, [t_ps[pd], t], [t_xo[s]])
                dma("sp", xT_d[m], xo[s], [t_xo[s]], [t_xT[m]])

        def phase_convbranch(full):
            vT = ar
            t_v = [Tile("v%d" % c) for c in range(32)]
            ccs = [arena_view(32768 + j * 2048, NT, F32) for j in range(2)]
            t_ccs = [Tile("ccs0"), Tile("ccs1")]
            ue = [arena_view(36864 + j * 2304, NT + 2, F32) for j in range(2)]
            t_ue = [Tile("ue0"), Tile("ue1")]
            acc = [arena_view(41984 + j * 2048, NT, F32) for j in range(2)]
            t_acc = [Tile("acc0"), Tile("acc1")]
            for c in range(32):
                pc, px, pbk = (0, 1, 2) if c % 2 == 0 else (3, 4, 5)
                s = c % 2
                gemm(psb[pc][:, :], t_ps[pc], [("win", O_CC + c * 128, 0)], hTk, lambda k: t_hT[k])
                gemm(psb[px][:, :], t_ps[px], [("win", O_CX + c * 128, 0)], hTk, lambda k: t_hT[k])
                cp(ccs[s], psb[pc][:, :], [t_ps[pc]], [t_ccs[s]], eng="act")
                cp(ue[s][:, 0:2], uh[:, c * 2:c * 2 + 2], [t_uh[c]], [t_ue[s]])
                tt(ue[s][:, 2:NT + 2], ccs[s], psb[px][:, :], ALU.mult, [t_ccs[s], t_ps[px], t_ue[s]], [t_ue[s]])
                cp(uh[:, c * 2:c * 2 + 2], ue[s][:, NT:NT + 2], [t_ue[s]], [t_uh[c]])
                if not full:
                    continue
                w0 = cv[:, CV_SCW + c:CV_SCW + c + 1]
                w1 = cv[:, CV_SCW + 32 + c:CV_SCW + 32 + c + 1]
                w2 = cv[:, CV_SCW + 64 + c:CV_SCW + 64 + c + 1]
                ts1(acc[s], ue[s][:, 2:NT + 2], w2, ALU.mult, [t_ue[s], t_cv], [t_acc[s]])
                stt(acc[s], ue[s][:, 1:NT + 1], w1, acc[s], ALU.mult, ALU.add, [t_ue[s], t_cv, t_acc[s]], [t_acc[s]])
                stt(acc[s], ue[s][:, 0:NT], w0, acc[s], ALU.mult, ALU.add, [t_ue[s], t_cv, t_acc[s]], [t_acc[s]])
                gemm(psb[pbk][:, :], t_ps[pbk], [("win", O_CB + c * 128, 0)], hTk, lambda k: t_hT[k])
                tt(vT[:, c * NT:(c + 1) * NT], acc[s], psb[pbk][:, :], ALU.mult, [t_acc[s], t_ps[pbk]], [t_v[c]])
            if not full:
                return
            sig = [arena_view(46080 + j * 2048, NT, F32) for j in range(2)]
            t_sig = [Tile("sig0"), Tile("sig1")]
            mao = [arena_view(50176 + j * 2048, NT, F32) for j in range(2)]
            t_mao = [Tile("mao0"), Tile("mao1")]
            for c in range(32):
                pg, py = (0, 1) if c % 2 == 0 else (2, 3)
                s = c % 2
                gemm(psb[pg][:, :], t_ps[pg], [("win", O_GA + c * 128, 0)], hTk, lambda k: t_hT[k])
                act(sig[s], psb[pg][:, :], AF.Sigmoid, [t_ps[pg], t_cv], [t_sig[s]], bias=cv[:, CV_GB + c:CV_GB + c + 1])
                gemm(psb[py][:, :], t_ps[py], [("sco", c * 128, 0)], lambda k: vT[:, k * NT:(k + 1) * NT], lambda k: t_v[k])
                tt(mao[s], sig[s], psb[py][:, :], ALU.mult, [t_sig[s], t_ps[py]], [t_mao[s]])
                dma("sp", mA_d[c], mao[s], [t_mao[s]], [t_mA[c]])


        class Alloc:
            def __init__(self):
                self.o = 0

            def __call__(self, nbytes, nelem, dt):
                o = self.o
                self.o += (nbytes + 63) // 64 * 64
                assert self.o <= XC_OFF, self.o
                return arena_view(o, nelem, dt)

        def bc(ap_sb, off, F, dims):
            raise NotImplementedError

        def phase_ssd(ti, full, first):
            al = Alloc()
            xs_tm = al(32768, 4 * 4096, BF16)
            BT = al(8192, 8 * NT, BF16)
            CT = al(8192, 8 * NT, BF16)
            B_tm = al(8192, 4 * 1024, BF16)
            dt = al(2048, 512, F32)
            der = [[al(512, 128, F32) for j in range(6)] for tc in range(4)]
            xe = [al(2304, NT + 3, F32) for j in range(2)]
            cacc = [al(2048, NT, F32) for j in range(2)]
            xsT = [al(1024, NT, BF16) for j in range(2)]
            Sg = al(4096, 1024, F32)
            Sbf = al(2048, 1024, BF16)
            cbT = al(512, 128, F32)
            Ls = [al(2048, 512, F32) for j in range(2)]
            scT = [al(1024, 512, BF16) for j in range(2)]
            xdt = al(2048, 1024, BF16)
            xdd = al(2048, 1024, BF16)
            yg = al(4096, 1024, F32)
            Dx = al(2048, 512, F32)
            t1 = al(2048, 512, F32)
            stg = [al(2048, 512, F32) for j in range(2)]
            tmp = [al(512, 128, F32) for j in range(4)]
            t_xs = [Tile("xs_tm%d" % c) for c in range(32)]
            t_BT = [Tile("BT%d" % g) for g in range(8)]
            t_CT = [Tile("CT%d" % g) for g in range(8)]
            t_Btm = [Tile("Btm%d" % g) for g in range(8)]
            t_dt = [Tile("dt%d" % tc) for tc in range(4)]
            t_der = [Tile("der%d" % tc) for tc in range(4)]
            t_xe = [Tile("xe0"), Tile("xe1")]
            t_cacc = [Tile("cacc0"), Tile("cacc1")]
            t_xsT = [Tile("xsT0"), Tile("xsT1")]
            t_Sg, t_Sbf, t_cbT = Tile("Sg"), Tile("Sbf"), Tile("cbT")
            t_Ls = [Tile("Ls0"), Tile("Ls1")]
            t_scT = [Tile("scT0"), Tile("scT1")]
            t_xdt, t_xdd, t_yg, t_Dx, t_t1 = Tile("xdt"), Tile("xdd"), Tile("yg"), Tile("Dx"), Tile("t1")
            t_stg = [Tile("stg0"), Tile("stg1")]
            t_tmp = Tile("tmp")
            cn = {"n": 0}

            def conv_chunk(c):
                n = cn["n"]
                cn["n"] += 1
                pp = n % 4
                s = n % 2
                gemm(psb[pp][:, :], t_ps[pp], [("win", O_XBC + c * 128, 0)], hTk, lambda k: t_hT[k])
                cp(xe[s][:, 0:3], xh[:, c * 3:c * 3 + 3], [t_xh[c]], [t_xe[s]])
                cp(xe[s][:, 3:NT + 3], psb[pp][:, :], [t_ps[pp], t_xe[s]], [t_xe[s]], eng="act")
                cp(xh[:, c * 3:c * 3 + 3], xe[s][:, NT:NT + 3], [t_xe[s]], [t_xh[c]])
                wk = [cv[:, CV_CW + k * 80 + c:CV_CW + k * 80 + c + 1] for k in range(4)]
                ts(cacc[s], xe[s][:, 3:NT + 3], wk[3], cv[:, CV_CB + c:CV_CB + c + 1], ALU.mult, ALU.add,
                   [t_xe[s], t_cv], [t_cacc[s]])
                for k in (2, 1, 0):
                    stt(cacc[s], xe[s][:, k:NT + k], wk[k], cacc[s], ALU.mult, ALU.add, [t_xe[s], t_cv, t_cacc[s]], [t_cacc[s]])
                return s

            def r4(a):
                return a.rearrange("p (t c) -> p t c", t=4)

            for c in range(64, 80):
                s = conv_chunk(c)
                if c < 72:
                    g = c - 64
                    act(BT[:, g * NT:(g + 1) * NT], cacc[s], AF.Silu, [t_cacc[s]], [t_BT[g]])
                    hb = c % 2
                    for tc in range(4):
                        tr(pstb[:, hb * 512 + tc * 128: hb * 512 + (tc + 1) * 128],
                           BT[:, g * NT + tc * 128: g * NT + (tc + 1) * 128], identb[:, :], [t_BT[g], t_const], [t_pstb[hb]])
                    cp(r4(B_tm)[:, :, g * 128:(g + 1) * 128], r4(pstb[:, hb * 512:(hb + 1) * 512]), [t_pstb[hb]], [t_Btm[g]])
                else:
                    g = c - 72
                    act(CT[:, g * NT:(g + 1) * NT], cacc[s], AF.Silu, [t_cacc[s]], [t_CT[g]])
            w, wt, kc = next_unit("win", O_DT, 0)
            for tc in range(4):
                pp = 4 + tc % 2
                for k in range(32):
                    mm(psb[pp][:, 0:128], hT[:, k * NT + tc * 128: k * NT + (tc + 1) * 128], w[:, k * 128:(k + 1) * 128],
                       k == 0, k == 31, [wt, t_hT[k]], [t_ps[pp]])
                d_ = dt[:, tc * 128:(tc + 1) * 128]
                rd, wr = [t_tmp], [t_tmp]
                tt(tmp[0], psb[pp][:, 0:128], dtb_b, ALU.add, [t_ps[pp], t_rb] + rd, wr)
                ts1(tmp[1], tmp[0], -1.0, ALU.mult, rd, wr)
                tt(tmp[1], tmp[1], tmp[0], ALU.min, rd, wr)
                act(tmp[2], tmp[1], AF.Exp, rd, wr)
                act(tmp[2], tmp[2], AF.Ln, rd, wr, bias=1.0)
                ts1(tmp[3], tmp[0], 0.0, ALU.max, rd, wr)
                tt(d_, tmp[3], tmp[2], ALU.add, rd, [t_tmp, t_dt[tc]])
                a_, acs, ea, dte, nacs, cdb = der[tc]
                dd = [t_der[tc]]
                tt(a_, d_, Ab[:, :], ALU.mult, [t_dt[tc], t_const], dd)
                p1, p2 = 0, 1
                mm(psb[p1][:, 0:128], tri[:, :], a_, True, True, dd + [t_const], [t_ps[p1]])
                mm(psb[p2][:, 0:128], onesf[:, :], a_, True, True, dd + [t_const], [t_ps[p2]])
                cp(acs, psb[p1][:, 0:128], [t_ps[p1]] + dd, dd, eng="act")
                act(ea, acs, AF.Exp, dd, dd)
                tt(dte, psb[p2][:, 0:128], acs, ALU.subtract, [t_ps[p2]] + dd, dd)
                act(dte, dte, AF.Exp, dd, dd)
                act(cdb, psb[p2][:, 0:128], AF.Exp, [t_ps[p2]] + dd, dd)
                ts1(nacs, acs, -1.0, ALU.mult, dd, dd)

            def b3(ap2d, n1, n2):
                a = ap2d
                return bass.AP(a.tensor, a.offset, [list(a.ap[0]), [a.ap[-1][0], n1], [0, n2]])

            def v3(ap2d, n1, n2):
                return ap2d.rearrange("p (a b) -> p a b", a=n1)

            def cb3(ap2d, n1):
                a = ap2d
                return bass.AP(a.tensor, a.offset, [list(a.ap[0]), [0, n1], list(a.ap[-1])])

            for half in range(2):
                for cl in range(32):
                    c = half * 32 + cl
                    s = conv_chunk(c)
                    act(xsT[s], cacc[s], AF.Silu, [t_cacc[s]], [t_xsT[s]])
                    hb = c % 2
                    for tc in range(4):
                        tr(pstb[:, hb * 512 + tc * 128: hb * 512 + (tc + 1) * 128], xsT[s][:, tc * 128:(tc + 1) * 128],
                           identb[:, :], [t_xsT[s], t_const], [t_pstb[hb]])
                    cp(r4(xs_tm)[:, :, cl * 128:(cl + 1) * 128], r4(pstb[:, hb * 512:(hb + 1) * 512]), [t_pstb[hb]], [t_xs[cl]])
                for gl in range(4):
                    g = half * 4 + gl
                    if first:
                        P.add("dve", lambda e: e.memset(Sg, 0.0), writes=[t_Sg])
                    else:
                        dma("sp", Sg, S_d[g], [t_S[g]], [t_Sg])
                    xs_tiles = t_xs[gl * 8:(gl + 1) * 8]
                    for tc in range(4):
                        a_, acs, ea, dte, nacs, cdb = der[tc]
                        dd = [t_der[tc]]
                        tok = slice(g * NT + tc * 128, g * NT + (tc + 1) * 128)
                        if full:
                            mm(psb[0][:, 0:128], BT[:, tok], CT[:, tok], True, True, [t_BT[g], t_CT[g]], [t_ps[0]])
                            cp(cbT, psb[0][:, 0:128], [t_ps[0]], [t_cbT], eng="act")
                            cp(Sbf, Sg, [t_Sg], [t_Sbf])
                            for h2 in range(2):
                                mm(psb[1 + h2][:, :], CT[:, tok], Sbf[:, h2 * 512:(h2 + 1) * 512], True, True,
                                   [t_CT[g], t_Sbf], [t_ps[1 + h2]])
                        xs_g = xs_tm[:, tc * 4096 + gl * 1024: tc * 4096 + (gl + 1) * 1024]
                        tt(v3(xdt, 16, 64), v3(xs_g, 16, 64), b3(dt[:, tc * 128 + g * 16: tc * 128 + (g + 1) * 16], 16, 64),
                           ALU.mult, xs_tiles + [t_dt[tc]], [t_xdt])
                        tt(v3(xdd, 16, 64), v3(xdt, 16, 64), b3(dte[:, g * 16:(g + 1) * 16], 16, 64), ALU.mult,
                           [t_xdt] + dd, [t_xdd])
                        for q4 in range(4 if full else 0):
                            s = q4 % 2
                            pl = 3 + s
                            for q in range(4):
                                h = g * 16 + q4 * 4 + q
                                mm(psb[pl][:, q * 128:(q + 1) * 128], a_[:, h:h + 1].to_broadcast([128, 128]), tri[:, :],
                                   True, False, dd + [t_const], [t_ps[pl]])
                                mm(psb[pl][:, q * 128:(q + 1) * 128], identf[:, :], maskf[:, :], False, True, [t_const], [t_ps[pl]])
                            for q in range(4):
                                h = g * 16 + q4 * 4 + q
                                act(Ls[s][:, q * 128:(q + 1) * 128], psb[pl][:, q * 128:(q + 1) * 128], AF.Exp,
                                    [t_ps[pl]] + dd, [t_Ls[s]], bias=nacs[:, h:h + 1])
                            tt(v3(scT[s], 4, 128), v3(Ls[s], 4, 128), cb3(cbT, 4), ALU.mult, [t_Ls[s], t_cbT], [t_scT[s]])
                            for q in range(4):
                                hl = q4 * 4 + q
                                py = 5 + hl // 8
                                mm(psb[py][:, (hl % 8) * 64:(hl % 8 + 1) * 64], scT[s][:, q * 128:(q + 1) * 128],
                                   xdt[:, hl * 64:(hl + 1) * 64], True, True, [t_scT[s], t_xdt], [t_ps[py]])
                        for h2 in range(2 if full else 0):
                            h0 = g * 16 + h2 * 8
                            tt(v3(t1, 8, 64), v3(psb[1 + h2][:, :], 8, 64), b3(ea[:, h0:h0 + 8], 8, 64), ALU.mult,
                               [t_ps[1 + h2]] + dd, [t_t1])
                            tt(v3(Dx, 8, 64), v3(xs_g[:, h2 * 512:(h2 + 1) * 512], 8, 64), b3(D_b[:, h0:h0 + 8], 8, 64), ALU.mult,
                               xs_tiles + [t_rb], [t_Dx])
                            tt(t1, t1, Dx, ALU.add, [t_t1, t_Dx], [t_t1])
                            tt(yg[:, h2 * 512:(h2 + 1) * 512], psb[5 + h2][:, :], t1, ALU.add, [t_ps[5 + h2], t_t1], [t_yg])
                        for h2 in range(2):
                            mm(psb[1 + h2][:, :], B_tm[:, tc * 1024 + g * 128: tc * 1024 + (g + 1) * 128],
                               xdd[:, h2 * 512:(h2 + 1) * 512], True, True, [t_Btm[g], t_xdd], [t_ps[1 + h2]])
                        tt(v3(Sg, 16, 64), v3(Sg, 16, 64), b3(cdb[:, g * 16:(g + 1) * 16], 16, 64), ALU.mult, [t_Sg] + dd, [t_Sg])
                        for h2 in range(2):
                            tt(Sg[:, h2 * 512:(h2 + 1) * 512], Sg[:, h2 * 512:(h2 + 1) * 512], psb[1 + h2][:, :], ALU.add,
                               [t_Sg, t_ps[1 + h2]], [t_Sg])
                        if full:
                            for c4 in range(2):
                                s = c4
                                for q in range(4):
                                    cc = c4 * 4 + q
                                    tr(psb[0][:, q * 128:(q + 1) * 128], yg[:, cc * 128:(cc + 1) * 128], identf[:, :],
                                       [t_yg, t_const], [t_ps[0]])
                                cp(stg[s], psb[0][:, :], [t_ps[0]], [t_stg[s]], eng="act")
                                c0 = g * 8 + c4 * 4
                                dma("sp", yT_d[c0:c0 + 4, :, tc * 128:(tc + 1) * 128].rearrange("c p t -> p c t"),
                                    r4(stg[s]), [t_stg[s]], t_yT[c0:c0 + 4])
                    dma("sp", S_d[g], Sg, [t_Sg], [t_S[g]])

        def phase_post():
            al = Alloc()
            ynT = al(65536, 64 * NT, BF16)
            mT = al(32768, 32 * NT, BF16)
            yz = al(16384, 8 * NT, F32)
            sz = [al(2048, NT, F32) for j in range(2)]
            t_yn = [Tile("yn%d" % c) for c in range(64)]
            t_m = [Tile("m%d" % c) for c in range(32)]
            t_yz = [Tile("yz%d" % c) for c in range(8)]
            t_sz = [Tile("sz0"), Tile("sz1")]
            for g in range(8):
                for cc in range(8):
                    c = g * 8 + cc
                    pp = c % 4
                    s = c % 2
                    gemm(psb[pp][:, :], t_ps[pp], [("win", O_Z + c * 128, 0)], hTk, lambda k: t_hT[k])
                    act(sz[s], psb[pp][:, :], AF.Silu, [t_ps[pp]], [t_sz[s]])
                    a, t = load_chunk(yT_d[c], t_yT[c])
                    tt(yz[:, cc * NT:(cc + 1) * NT], sz[s], a, ALU.mult, [t_sz[s], t], [t_yz[cc]])
                    s2 = cnt["sq"] % 2
                    cnt["sq"] += 1
                    act(sq[s2], yz[:, cc * NT:(cc + 1) * NT], AF.Square, [t_yz[cc]], [t_sq[s2]])
                    mm(psb[6][:, :], onesb[:, :], sq[s2], cc == 0, cc == 7, [t_sq[s2], t_const], [t_ps[6]])
                ts(rstd, psb[6][:, :], 1.0 / 1024, EPS, ALU.mult, ALU.add, [t_ps[6]], [t_rstd])
                act(rstd, rstd, AF.Sqrt, [t_rstd], [t_rstd])
                P.add("dve", lambda e: e.reciprocal(out=rstd, in_=rstd), reads=[t_rstd], writes=[t_rstd])
                for cc in range(8):
                    c = g * 8 + cc
                    stt(ynT[:, c * NT:(c + 1) * NT], yz[:, cc * NT:(cc + 1) * NT], cv[:, CV_SN + c:CV_SN + c + 1], rstd,
                        ALU.mult, ALU.mult, [t_yz[cc], t_rstd, t_cv], [t_yn[c]])
            sig = [yz[:, j * NT:(j + 1) * NT] for j in range(2)]
            tmpm = [yz[:, (2 + j) * NT:(3 + j) * NT] for j in range(2)]
            t_sig = [t_yz[0], t_yz[1]]
            t_tm = [t_yz[2], t_yz[3]]
            for c in range(32):
                pg, py = (0, 1) if c % 2 == 0 else (2, 3)
                s = c % 2
                gemm(psb[pg][:, :], t_ps[pg], [("win", O_GB + c * 128, 0)], hTk, lambda k: t_hT[k])
                act(sig[s], psb[pg][:, :], AF.Sigmoid, [t_ps[pg], t_cv], [t_sig[s]], bias=cv[:, CV_GB + 32 + c:CV_GB + 32 + c + 1])
                gemm(psb[py][:, :], t_ps[py], [("sso", c * 128, 0), ("sso", c * 128, 32)],
                     lambda k: ynT[:, k * NT:(k + 1) * NT], lambda k: t_yn[k])
                a, t = load_chunk(mA_d[c], t_mA[c])
                tt(tmpm[s], sig[s], psb[py][:, :], ALU.mult, [t_sig[s], t_ps[py]], [t_tm[s]])
                tt(mT[:, c * NT:(c + 1) * NT], tmpm[s], a, ALU.add, [t_tm[s], t], [t_m[c]])
            xo = [sz[0], sz[1]]
            t_xo = t_sz
            for c in range(32):
                pp = 4 + c % 2
                gemm(psb[pp][:, :], t_ps[pp], [("wo", c * 128, 0)], lambda k: mT[:, k * NT:(k + 1) * NT], lambda k: t_m[k])
                a, t = load_chunk(xT_d[c], t_xT[c])
                s = c % 2
                tt(xo[s], psb[pp][:, :], a, ALU.add, [t_ps[pp], t], [t_xo[s]])
                dma("sp", xT_d[c], xo[s], [t_xo[s]], [t_xT[c]])

        def dbg_dump(i):
            if debug:
                P.barrier()
                td = Tile("dbg%d" % i)
                t_out.append(td)
                dma("sp", dbg_d[i], xT_d, t_xT, [td])
                P.barrier()

        def phase_mask():
            buf = [arena_view(j * 4096, 1024, F32) for j in range(2)]
            t_buf = [Tile("mb0"), Tile("mb1")]
            for g in range(8):
                s = g % 2
                dma("sp", buf[s], S_d[g], [t_S[g]], [t_buf[s]])
                ts1(buf[s], buf[s], smk[:, 0:1], ALU.mult, [t_buf[s], t_rb], [t_buf[s]])
                dma("sp", S_d[g], buf[s], [t_buf[s]], [t_S[g]])
            ts1(xh[:, :], xh[:, :], smk[:, 0:1], ALU.mult, t_xh + [t_rb], t_xh)
            ts1(uh[:, :], uh[:, :], smk[:, 0:1], ALU.mult, t_uh + [t_rb], t_uh)

        nfull_seen = 0
        for ti, mode in enumerate(modes):
            full = mode == "full"
            if full and ti > 0 and modes[ti - 1] == "prefix":
                P.barrier()
                phase_mask()
            wstate["cur"] = (ulists[mode], ubase[mode], 0)
            wstate["i"] = 0
            P.barrier()
            phase_load(ti)
            P.barrier()
            if upto <= 0:
                dbg_dump(0)
                break
            phase_norm(CV_N1)
            P.barrier()
            if upto <= 1:
                dbg_dump(0)
                break
            phase_ffn(1)
            if ti == 0:
                dbg_dump(0)
            if upto <= 2:
                break
            phase_norm(CV_NM)
            P.barrier()
            phase_convbranch(full)
            P.barrier()
            if upto <= 3:
                break
            phase_ssd(ti, full, ti == 0)
            if upto <= 4:
                P.barrier()
                td = Tile("dby")
                t_out.append(td)
                dma("sp", dby_d, yT_d, t_yT, [td])
                break
            if full:
                if debug and ti == 0:
                    P.barrier()
                    td = Tile("dby")
                    t_out.append(td)
                    dma("sp", dby_d, yT_d, t_yT, [td])
                P.barrier()
                phase_post()
                if ti == 0:
                    dbg_dump(1)
                if upto <= 5:
                    break
                phase_norm(CV_N2)
                P.barrier()
                phase_ffn(2)
                if ti == 0:
                    dbg_dump(2)
                if upto <= 6:
                    break
                P.barrier()
                phase_norm(CV_NF, final=True, ti_out=nfull_seen)
                nfull_seen += 1
            assert upto < 99 or wstate["i"] == len(ulists[mode]), (wstate["i"], len(ulists[mode]))
        P.emit(final_tiles=t_out)
    return nc, nu, ulists, ubase


def _colvec(v, nchunk):
    return np.ascontiguousarray(np.asarray(v, np.float32).reshape(nchunk, 128).T)


def make_cvec(inp):
    cvh = np.zeros((128, NCV), np.float32)
    cvh[:, CV_N1:CV_N1 + 32] = _colvec(inp["ffn1_norm"][0], 32)
    cvh[:, CV_NM:CV_NM + 32] = _colvec(inp["mix_norm"][0], 32)
    cvh[:, CV_N2:CV_N2 + 32] = _colvec(inp["ffn2_norm"][0], 32)
    cvh[:, CV_NF:CV_NF + 32] = _colvec(inp["final_norm"], 32)
    cvh[:, CV_GB:CV_GB + 64] = _colvec(inp["gate_bias"][0], 64)
    for k in range(3):
        cvh[:, CV_SCW + k * 32:CV_SCW + (k + 1) * 32] = _colvec(inp["sconv_w"][0, k], 32)
    for k in range(4):
        cvh[:, CV_CW + k * 80:CV_CW + (k + 1) * 80] = _colvec(inp["ssm_conv_w"][0, k], 80)
    cvh[:, CV_CB:CV_CB + 80] = _colvec(inp["ssm_conv_b"][0], 80)
    cvh[:, CV_SN:CV_SN + 64] = _colvec(inp["ssm_norm"][0], 64)
    rbh = np.zeros((128, 384), np.float32)
    rbh[:, 0:128] = np.asarray(inp["ssm_dt_bias"][0], np.float32)[None, :]
    rbh[:, 128:256] = np.asarray(inp["ssm_A_log"][0], np.float32)[None, :]
    rbh[:, 256:384] = np.asarray(inp["ssm_D"][0], np.float32)[None, :]
    return cvh, rbh


def make_wstream(inp, nu, ulists, ubase):
    W = {"g1": inp["ffn1_w_gate"][0], "u1": inp["ffn1_w_up"][0], "d1": inp["ffn1_w_down"][0],
         "g2": inp["ffn2_w_gate"][0], "u2": inp["ffn2_w_up"][0], "d2": inp["ffn2_w_down"][0],
         "win": inp["w_in"][0], "sco": inp["sconv_w_out"][0], "sso": inp["ssm_w_out"][0], "wo": inp["w_o"][0]}
    W = {k: np.asarray(v, np.float32) for k, v in W.items()}
    ws = np.zeros((nu, 128, 4096), np.float32)
    for m, ul in ulists.items():
        b = ubase[m]
        for i, (name, col0, k0, kc) in enumerate(ul):
            blk = W[name][k0 * 128:(k0 + kc) * 128, col0:col0 + 128]
            ws[b + i, :, :kc * 128] = blk.reshape(kc, 128, 128).transpose(1, 0, 2).reshape(128, kc * 128)
    return ws


_CACHE = {}


def kernel(**inp):
    x = np.asarray(inp["x"], np.float32)
    modes = ["prefix", "prefix", "full", "full"]
    key = "p2f2"
    if key not in _CACHE:
        _CACHE[key] = build(modes)
    nc, nu, ulists, ubase = _CACHE[key]
    cvh, rbh = make_cvec(inp)
    ws = make_wstream(inp, nu, {"full": ulists["full"]}, ubase)
    half = SEQ // 2
    in_maps = []
    for core in range(8):
        b, second = core // 2, core % 2
        if second:
            xc_ = np.ascontiguousarray(x[b])
        else:
            xc_ = np.ascontiguousarray(np.concatenate([x[b, :half], x[b, :half]], axis=0))
        smh = np.full((128, 1), float(second), np.float32)
        in_maps.append({"x": xc_, "ws": ws, "cv": cvh, "rb": rbh, "sm": smh})
    res = run_bass_kernel_spmd(nc, in_maps, core_ids=list(range(8)))
    out = np.empty((4, SEQ, D), np.float32)
    for core in range(8):
        b, second = core // 2, core % 2
        out[b, second * half:(second + 1) * half] = np.asarray(res.results[core]["out"], np.float32)
    return out
```

```python
from contextlib import ExitStack
import numpy as np
import concourse.bass as bass
import concourse.mybir as mybir
from concourse.bass_utils import run_bass_kernel_spmd

F32 = mybir.dt.float32
BF16 = mybir.dt.bfloat16
AF = mybir.ActivationFunctionType
ALU = mybir.AluOpType

D = 4096
DFF = 11008
NFF = 86
SEQ = 2048
NT = 512
EPS = 1e-5
O_CB, O_CC, O_CX, O_Z, O_XBC, O_DT, O_GA, O_GB = 0, 4096, 8192, 12288, 20480, 30720, 30848, 34944
CV_N1, CV_NM, CV_N2, CV_NF, CV_GB, CV_SCW, CV_CW, CV_CB, CV_SN, NCV = 0, 32, 64, 96, 128, 192, 288, 608, 688, 752

ENGS = ("pe", "act", "dve", "pool", "sp")
RING = {"pool": 8, "sp": 16, "act": 8}


class Tile:
    __slots__ = ("name", "w", "wd", "r", "rd", "x", "xl")

    def __init__(self, name, excl=False):
        self.name = name
        self.x = excl
        self.xl = {}
        self.w = {}
        self.wd = []
        self.r = {}
        self.rd = []


class Op:
    __slots__ = ("eng", "fn", "is_dma", "deps", "signal", "tok_val", "sem", "idx")


class Prog:
    def __init__(self, nc, stack):
        self.nc = nc
        self.ops = {e: [] for e in ENGS}
        self.n = 0
        self.esem = {e: stack.enter_context(nc.semaphore("es_" + e)) for e in ENGS}
        self.ring = {}
        for q, n in RING.items():
            self.ring[q] = [[stack.enter_context(nc.semaphore("rs_%s%d" % (q, i))), 0, None] for i in range(n)]
        self.rpos = {q: 0 for q in RING}
        self.bar = []
        self.bar_seen = set()

    def barrier(self):
        last = []
        for e in ("pe", "act", "dve"):
            for op in reversed(self.ops[e]):
                if (not op.is_dma) and op.fn is not None:
                    last.append(op)
                    break
        for q in ("sp", "act"):
            for slot in self.ring[q]:
                if slot[2] is not None:
                    last.append(slot[2])
        self.bar = last
        self.bar_seen = set()

    def add(self, eng, fn, reads=(), writes=(), dma=False):
        op = Op()
        op.eng, op.fn, op.is_dma, op.signal, op.tok_val, op.sem = eng, fn, dma, False, None, None
        op.idx = self.n
        self.n += 1
        deps = {}
        for t in reads:
            for w in t.w.values():
                deps[w.idx] = w
            for w in t.wd:
                deps[w.idx] = w
        for t in writes:
            for w in t.w.values():
                deps[w.idx] = w
            for w in t.wd:
                deps[w.idx] = w
            for r in t.r.values():
                deps[r.idx] = r
            for r in t.rd:
                deps[r.idx] = r
        for t in list(reads) + list(writes):
            if t.x:
                for e2, o2 in t.xl.items():
                    if e2 != eng:
                        deps[o2.idx] = o2
                t.xl[eng] = op
        if eng != "pool" and self.bar and eng not in self.bar_seen:
            self.bar_seen.add(eng)
            for b in self.bar:
                deps[b.idx] = b
        if dma:
            slot = self.ring[eng][self.rpos[eng] % len(self.ring[eng])]
            self.rpos[eng] += 1
            if slot[2] is not None:
                deps[slot[2].idx] = slot[2]
            slot[1] += 16
            slot[2] = op
            op.sem, op.tok_val = slot[0], slot[1]
        for t in reads:
            if dma:
                t.rd.append(op)
            else:
                t.r[eng] = op
        for t in writes:
            t.w, t.wd, t.r, t.rd = {}, [], {}, []
            if dma:
                t.wd.append(op)
            else:
                t.w[eng] = op
        dl = []
        for d in deps.values():
            if (not d.is_dma) and (not dma) and d.eng == "pe" and eng == "pe":
                continue
            if not d.is_dma:
                d.signal = True
            dl.append(d)
        op.deps = dl
        self.ops[eng].append(op)
        return op

    def emit(self, final_tiles=()):
        nc = self.nc
        self.add("sp", None, reads=list(final_tiles))
        for e in ENGS:
            c = 0
            for op in self.ops[e]:
                if (not op.is_dma) and op.signal:
                    c += 1
                    op.tok_val = c
        prog = self

        def run(e, eng):
            waited = {}
            for op in prog.ops[e]:
                for d in op.deps:
                    sem, val = (d.sem, d.tok_val) if d.is_dma else (prog.esem[d.eng], d.tok_val)
                    k = id(sem)
                    if waited.get(k, 0) >= val:
                        continue
                    waited[k] = val
                    eng.wait_ge(sem, val)
                if op.fn is None:
                    continue
                inst = op.fn(eng)
                if op.is_dma:
                    inst.then_inc(op.sem, 16)
                elif op.signal:
                    inst.then_inc(prog.esem[e], 1)

        with nc.Block() as block:
            @block.tensor
            def _(eng):
                run("pe", eng)

            @block.scalar
            def _(eng):
                run("act", eng)

            @block.vector
            def _(eng):
                run("dve", eng)

            @block.gpsimd
            def _(eng):
                run("pool", eng)

            @block.sync
            def _(eng):
                run("sp", eng)


class SB:
    def __init__(self, h, F):
        self.h, self.F = h, F

    def __getitem__(self, key):
        return self.h[key]

    def v(self, off, dims, npart=128):
        return bass.AP(self.h, off, [[self.F, npart]] + [list(d) for d in dims])


def unit_list(mode="full"):
    u = []

    def ffn(i):
        for m in range(NFF):
            u.append(("g%d" % i, m * 128, 0, 32))
            u.append(("u%d" % i, m * 128, 0, 32))
        for m in range(32):
            u.append(("d%d" % i, m * 128, 0, 32))
            u.append(("d%d" % i, m * 128, 32, 32))
            u.append(("d%d" % i, m * 128, 64, 22))

    ffn(1)
    if mode == "full":
        for c in range(32):
            u.append(("win", O_CC + c * 128, 0, 32))
            u.append(("win", O_CX + c * 128, 0, 32))
            u.append(("win", O_CB + c * 128, 0, 32))
        for c in range(32):
            u.append(("win", O_GA + c * 128, 0, 32))
            u.append(("sco", c * 128, 0, 32))
    elif mode == "prefix":
        for c in range(32):
            u.append(("win", O_CC + c * 128, 0, 32))
            u.append(("win", O_CX + c * 128, 0, 32))
    for c in range(64, 80):
        u.append(("win", O_XBC + c * 128, 0, 32))
    u.append(("win", O_DT, 0, 32))
    for c in range(64):
        u.append(("win", O_XBC + c * 128, 0, 32))
    if mode == "full":
        for c in range(64):
            u.append(("win", O_Z + c * 128, 0, 32))
        for c in range(32):
            u.append(("win", O_GB + c * 128, 0, 32))
            u.append(("sso", c * 128, 0, 32))
            u.append(("sso", c * 128, 32, 32))
        for c in range(32):
            u.append(("wo", c * 128, 0, 32))
        ffn(2)
    return u


def build(modes, debug=False, upto=99):
    ntile = len(modes)
    nfull = sum(1 for m in modes if m == "full")
    ulists = {m: unit_list(m) for m in set(modes) | {"full"}}
    ubase = {m: 0 for m in ulists}
    nu = len(ulists["full"])
    uindex = {(n_, c_, k_): i for i, (n_, c_, k_, kc_) in enumerate(ulists["full"])}

    nc = bass.Bass("TRN2", target_bir_lowering=False)
    x_d = nc.dram_tensor("x", [ntile * NT, D], F32, kind="ExternalInput").ap()
    ws_d = nc.dram_tensor("ws", [nu, 128, 4096], F32, kind="ExternalInput").ap()
    cv_d = nc.dram_tensor("cv", [128, NCV], F32, kind="ExternalInput").ap()
    rb_d = nc.dram_tensor("rb", [128, 384], F32, kind="ExternalInput").ap()
    sm_d = nc.dram_tensor("sm", [128, 1], F32, kind="ExternalInput").ap()
    out_d = nc.dram_tensor("out", [nfull * NT, D], F32, kind="ExternalOutput").ap()
    xT_d = nc.dram_tensor("xT_s", [32, 128, NT], F32).ap()
    mA_d = nc.dram_tensor("mA_s", [32, 128, NT], F32).ap()
    yT_d = nc.dram_tensor("yT_s", [64, 128, NT], F32).ap()
    S_d = nc.dram_tensor("S_s", [8, 128, 1024], F32).ap()
    dbg_d = None
    if debug:
        dbg_d = nc.dram_tensor("dbg", [4, 32, 128, NT], F32, kind="ExternalOutput").ap()
        dby_d = nc.dram_tensor("dby", [64, 128, NT], F32, kind="ExternalOutput").ap()

    with ExitStack() as st:
        P = Prog(nc, st)

        def sb(name, F, dt):
            return SB(st.enter_context(nc.sbuf_tensor("s_" + name, [128, F], dt)), F)

        def pst(name, F, dt):
            return SB(st.enter_context(nc.psum_tensor("p_" + name, [128, F], dt)), F)

        cv = sb("cv", NCV, F32)
        rb = sb("rb", 384, F32)
        smk = sb("smk", 1, F32)
        identf = sb("identf", 128, F32)
        identb = sb("identb", 128, BF16)
        tri = sb("tri", 128, F32)
        maskf = sb("maskf", 128, F32)
        onesf = sb("onesf", 128, F32)
        onesb = sb("onesb", 128, BF16)
        Ab = sb("Ab", 128, F32)
        xh = sb("xh", 80 * 3, F32)
        uh = sb("uh", 32 * 2, F32)
        NSLOT = 5
        wsl = [sb("wsl%d" % i, 4096, BF16) for i in range(NSLOT)]
        hT = sb("hT", 32 * NT, BF16)
        ARENA = 65024
        ar = sb("arena", ARENA, BF16)
        arf = SB(ar.h, ARENA)

        def arena_view(off_bytes, nelem, dt):
            if dt == BF16:
                a = ar[:, off_bytes // 2: off_bytes // 2 + nelem]
            else:
                a = ar[:, off_bytes // 2: off_bytes // 2 + nelem * 2].bitcast(F32)
            return a

        psb = [pst("ps%d" % i, 512, F32) for i in range(7)]
        pstb = pst("pstb", 1024, BF16)
        t_ps = [Tile("ps%d" % i, True) for i in range(7)]
        _tp = Tile("pstb", True)
        t_pstb = [_tp, _tp]

        t_cv, t_rb, t_const = Tile("cv"), Tile("rb"), Tile("const")
        t_wsl = [Tile("wsl%d" % i) for i in range(NSLOT)]
        t_hT = [Tile("hT%d" % i) for i in range(32)]
        t_xT = [Tile("xT%d" % i) for i in range(32)]
        t_mA = [Tile("mA%d" % i) for i in range(32)]
        t_yT = [Tile("yT%d" % i) for i in range(64)]
        t_S = [Tile("S%d" % i) for i in range(8)]
        t_xh = [Tile("xh%d" % i) for i in range(80)]
        t_uh = [Tile("uh%d" % i) for i in range(32)]
        t_out = []

        def mm(ps, lhsT, rhs, start, stop, reads, writes):
            P.add("pe", lambda e: e.matmul(ps, lhsT=lhsT, rhs=rhs, start=start, stop=stop), reads=reads, writes=writes)

        def tr(ps, in_, ident, reads, writes):
            P.add("pe", lambda e: e.transpose(ps, in_, ident), reads=reads, writes=writes)

        def act(out, in_, func, reads, writes, bias=0.0, scale=1.0):
            P.add("act", lambda e: e.activation(out=out, in_=in_, func=func, bias=bias, scale=scale), reads=reads, writes=writes)

        def tt(out, in0, in1, op, reads, writes, eng="dve"):
            P.add(eng, lambda e: e.tensor_tensor(out=out, in0=in0, in1=in1, op=op), reads=reads, writes=writes)

        def ts(out, in0, s1, s2, op0, op1, reads, writes, eng="dve"):
            P.add(eng, lambda e: e.tensor_scalar(out=out, in0=in0, scalar1=s1, scalar2=s2, op0=op0, op1=op1), reads=reads, writes=writes)

        def ts1(out, in0, s1, op0, reads, writes, eng="dve"):
            P.add(eng, lambda e: e.tensor_single_scalar(out=out, in_=in0, scalar=s1, op=op0), reads=reads, writes=writes)

        def stt(out, in0, scalar, in1, op0, op1, reads, writes, eng="dve"):
            P.add(eng, lambda e: e.scalar_tensor_tensor(out=out, in0=in0, scalar=scalar, in1=in1, op0=op0, op1=op1), reads=reads, writes=writes)

        def cp(out, in_, reads, writes, eng="dve"):
            if eng == "act":
                P.add("act", lambda e: e.copy(out=out, in_=in_), reads=reads, writes=writes)
            else:
                P.add(eng, lambda e: e.tensor_copy(out=out, in_=in_), reads=reads, writes=writes)

        def dma(q, out, in_, reads, writes):
            P.add(q, lambda e: e.dma_start(out=out, in_=in_), reads=reads, writes=writes, dma=True)

        dma("sp", cv[:, :], cv_d, [], [t_cv])
        dma("sp", rb[:, :], rb_d, [], [t_rb])
        dma("sp", smk[:, :], sm_d, [], [t_rb])
        cst = [t_const]
        P.add("pool", lambda e: e.memset(identf[:, :], 1.0), writes=cst)
        P.add("pool", lambda e: e.affine_select(out=identf[:, :], in_=identf[:, :], pattern=[[1, 128]], compare_op=ALU.is_equal,
                                                fill=0.0, base=0, channel_multiplier=-1), reads=cst, writes=cst)
        P.add("pool", lambda e: e.tensor_copy(out=identb[:, :], in_=identf[:, :]), reads=cst, writes=cst)
        P.add("pool", lambda e: e.memset(tri[:, :], 1.0), reads=cst, writes=cst)
        P.add("pool", lambda e: e.affine_select(out=tri[:, :], in_=tri[:, :], pattern=[[1, 128]], compare_op=ALU.is_ge,
                                                fill=0.0, base=0, channel_multiplier=-1), reads=cst, writes=cst)
        P.add("pool", lambda e: e.memset(maskf[:, :], 0.0), reads=cst, writes=cst)
        P.add("pool", lambda e: e.affine_select(out=maskf[:, :], in_=maskf[:, :], pattern=[[1, 128]], compare_op=ALU.is_ge,
                                                fill=-30000.0, base=0, channel_multiplier=-1), reads=cst, writes=cst)
        P.add("pool", lambda e: e.memset(onesf[:, :], 1.0), reads=cst, writes=cst)
        P.add("pool", lambda e: e.memset(onesb[:, :], 1.0), reads=cst, writes=cst)
        P.add("pool", lambda e: e.memset(xh[:, :], 0.0), writes=t_xh)
        P.add("pool", lambda e: e.memset(uh[:, :], 0.0), writes=t_uh)
        act(Ab[:, :], rb[:, 128:256], AF.Exp, [t_rb], cst)
        ts1(Ab[:, :], Ab[:, :], -1.0, ALU.mult, cst, cst)
        dtb_b = rb[:, 0:128]
        D_b = rb[:, 256:384]
        CRD = [t_cv, t_rb, t_const]

        wstate = {"slot": 0, "cur": None, "i": 0}

        def next_unit(expect_name, expect_col, expect_k0):
            ul, base, i = wstate["cur"]
            name, col0, k0, kc = ul[wstate["i"]]
            assert (name, col0, k0) == (expect_name, expect_col, expect_k0), (name, col0, k0, expect_name, expect_col, expect_k0)
            s = wstate["slot"] % NSLOT
            wstate["slot"] += 1
            uidx = uindex[(name, col0, k0)]
            wstate["i"] += 1
            dma("pool", wsl[s][:, 0:kc * 128], ws_d[uidx][:, 0:kc * 128], [], [t_wsl[s]])
            return wsl[s], t_wsl[s], kc

        def gemm(ps, pt, units, rhs_of_k, rt_of_k, extra_reads=()):
            loaded = []
            tot = 0
            for (name, col0, k0) in units:
                w, wt, kc = next_unit(name, col0, k0)
                loaded.append((w, wt, kc, k0))
                tot += kc
            i = 0
            for (w, wt, kc, k0) in loaded:
                for kk in range(kc):
                    mm(ps, w[:, kk * 128:(kk + 1) * 128], rhs_of_k(k0 + kk), i == 0, i == tot - 1,
                       [wt, rt_of_k(k0 + kk)] + list(extra_reads), [pt])
                    i += 1

        def hTk(k):
            return hT[:, k * NT:(k + 1) * NT]

        def phase_load(ti):
            xin = [arena_view(i * 16384, 4096, F32) for i in range(2)]
            t_xin = [Tile("xin0"), Tile("xin1")]
            stg = [arena_view(32768 + i * 2048, 512, F32) for i in range(4)]
            t_stg = [Tile("stg%d" % i) for i in range(4)]
            n = 0
            for tc in range(4):
                b = tc % 2
                dma("sp", xin[b], x_d[ti * NT + tc * 128: ti * NT + (tc + 1) * 128, :], [], [t_xin[b]])
                for c4 in range(8):
                    pb = 4 + (n % 2)
                    for q in range(4):
                        c = c4 * 4 + q
                        tr(psb[pb][:, q * 128:(q + 1) * 128], xin[b][:, c * 128:(c + 1) * 128], identf[:, :],
                           [t_xin[b], t_const], [t_ps[pb]])
                    s = n % 4
                    cp(stg[s], psb[pb][:, :], [t_ps[pb]], [t_stg[s]], eng=("act" if n % 2 else "dve"))
                    dma("sp", xT_d[c4 * 4:(c4 + 1) * 4, :, tc * 128:(tc + 1) * 128].rearrange("c p t -> p c t"),
                        stg[s].rearrange("p (c t) -> p c t", c=4), [t_stg[s]], t_xT[c4 * 4:(c4 + 1) * 4])
                    n += 1

        def norm_stats(nelem_inv, src_load, nchunks, ps_i, tag):
            raise NotImplementedError

        XC_OFF = 119808
        xc = [arena_view(XC_OFF + i * 2048, NT, F32) for i in range(3)]
        t_xc = [Tile("xc%d" % i) for i in range(3)]
        sq = [arena_view(XC_OFF + 6144 + i * 1024, NT, BF16) for i in range(2)]
        t_sq = [Tile("sq0"), Tile("sq1")]
        rstd = arena_view(XC_OFF + 8192, NT, F32)
        t_rstd = Tile("rstd")

        cnt = {"xc": 0, "sq": 0}

        def load_chunk(src_ap, src_tile):
            i = cnt["xc"] % 3
            cnt["xc"] += 1
            dma("sp", xc[i], src_ap, [src_tile], [t_xc[i]])
            return xc[i], t_xc[i]

        def phase_norm(gcol, final=False, ti_out=None):
            pss, tss = psb[6], t_ps[6]
            for c in range(32):
                a, t = load_chunk(xT_d[c], t_xT[c])
                s = cnt["sq"] % 2
                cnt["sq"] += 1
                act(sq[s], a, AF.Square, [t], [t_sq[s]])
                mm(pss[:, :], onesb[:, :], sq[s], c == 0, c == 31, [t_sq[s], t_const], [tss])
            ts(rstd, pss[:, :], 1.0 / D, EPS, ALU.mult, ALU.add, [tss], [t_rstd])
            act(rstd, rstd, AF.Sqrt, [t_rstd], [t_rstd])
            P.add("dve", lambda e: e.reciprocal(out=rstd, in_=rstd), reads=[t_rstd], writes=[t_rstd])
            if not final:
                for c in range(32):
                    a, t = load_chunk(xT_d[c], t_xT[c])
                    stt(hTk(c), a, cv[:, gcol + c:gcol + c + 1], rstd, ALU.mult, ALU.mult, [t, t_rstd, t_cv], [t_hT[c]])
                return
            otm = [arena_view(tc * 16384, 4096, F32) for tc in range(4)]
            t_otm = [Tile("otm%d" % i) for i in range(4)]
            on = [arena_view(65536 + i * 2048, NT, F32) for i in range(2)]
            t_on = [Tile("on0"), Tile("on1")]
            for c in range(32):
                a, t = load_chunk(xT_d[c], t_xT[c])
                s = c % 2
                stt(on[s], a, cv[:, gcol + c:gcol + c + 1], rstd, ALU.mult, ALU.mult, [t, t_rstd, t_cv], [t_on[s]])
                pb = 4 + (c % 2)
                if upto == 98:
                    continue
                for tc in range(4):
                    tr(psb[pb][:, tc * 128:(tc + 1) * 128], on[s][:, tc * 128:(tc + 1) * 128], identf[:, :],
                       [t_on[s], t_const], [t_ps[pb]])
                if upto == 97:
                    continue
                for tc in range(4):
                    cp(otm[tc][:, c * 128:(c + 1) * 128], psb[pb][:, tc * 128:(tc + 1) * 128], [t_ps[pb]], [t_otm[tc]],
                       eng=("act" if tc % 2 else "dve"))
            for tc in range(4):
                to = Tile("out")
                t_out.append(to)
                dma("sp", out_d[ti_out * NT + tc * 128: ti_out * NT + (tc + 1) * 128, :], otm[tc], [t_otm[tc]], [to])

        def phase_ffn(i):
            actT = ar
            t_act = [Tile("act%d" % m) for m in range(NFF)]
            sg = [arena_view(88 * 1024 + j * 2048, NT, F32) for j in range(2)]
            t_sg = [Tile("sg0"), Tile("sg1")]
            xo = [arena_view(92 * 1024 + j * 2048, NT, F32) for j in range(2)]
            t_xo = [Tile("xo0"), Tile("xo1")]
            for m in range(NFF):
                pg, pu = (0, 1) if m % 2 == 0 else (2, 3)
                gemm(psb[pg][:, :], t_ps[pg], [("g%d" % i, m * 128, 0)], hTk, lambda k: t_hT[k])
                gemm(psb[pu][:, :], t_ps[pu], [("u%d" % i, m * 128, 0)], hTk, lambda k: t_hT[k])
                s = m % 2
                act(sg[s], psb[pg][:, :], AF.Silu, [t_ps[pg]], [t_sg[s]])
                tt(actT[:, m * NT:(m + 1) * NT], sg[s], psb[pu][:, :], ALU.mult, [t_sg[s], t_ps[pu]], [t_act[m]])
            for m in range(32):
                pd = m % 4
                gemm(psb[pd][:, :], t_ps[pd], [("d%d" % i, m * 128, 0), ("d%d" % i, m * 128, 32), ("d%d" % i, m * 128, 64)],
                     lambda k: actT[:, k * NT:(k + 1) * NT], lambda k: t_act[k])
                a, t = load_chunk(xT_d[m], t_xT[m])
                s = m % 2
                stt(xo[s], psb[pd][:, :], 0.5, a, ALU.mult, ALU.add, [t_ps[pd], t], [t_xo[s]])
                dma("sp", xT_d[m], xo[s], [t_xo[s]], [t_xT[m]])

        def phase_convbranch(full):
            vT = ar
            t_v = [Tile("v%d" % c) for c in range(32)]
            ccs = [arena_view(32768 + j * 2048, NT, F32) for j in range(2)]
            t_ccs = [Tile("ccs0"), Tile("ccs1")]
            ue = [arena_view(36864 + j * 2304, NT + 2, F32) for j in range(2)]
            t_ue = [Tile("ue0"), Tile("ue1")]
            acc = [arena_view(41984 + j * 2048, NT, F32) for j in range(2)]
            t_acc = [Tile("acc0"), Tile("acc1")]
            for c in range(32):
                pc, px, pbk = (0, 1, 2) if c % 2 == 0 else (3, 4, 5)
                s = c % 2
                gemm(psb[pc][:, :], t_ps[pc], [("win", O_CC + c * 128, 0)], hTk, lambda k: t_hT[k])
                gemm(psb[px][:, :], t_ps[px], [("win", O_CX + c * 128, 0)], hTk, lambda k: t_hT[k])
                cp(ccs[s], psb[pc][:, :], [t_ps[pc]], [t_ccs[s]], eng="act")
                cp(ue[s][:, 0:2], uh[:, c * 2:c * 2 + 2], [t_uh[c]], [t_ue[s]])
                tt(ue[s][:, 2:NT + 2], ccs[s], psb[px][:, :], ALU.mult, [t_ccs[s], t_ps[px], t_ue[s]], [t_ue[s]])
                cp(uh[:, c * 2:c * 2 + 2], ue[s][:, NT:NT + 2], [t_ue[s]], [t_uh[c]])
                if not full:
                    continue
                w0 = cv[:, CV_SCW + c:CV_SCW + c + 1]
                w1 = cv[:, CV_SCW + 32 + c:CV_SCW + 32 + c + 1]
                w2 = cv[:, CV_SCW + 64 + c:CV_SCW + 64 + c + 1]
                ts1(acc[s], ue[s][:, 2:NT + 2], w2, ALU.mult, [t_ue[s], t_cv], [t_acc[s]])
                stt(acc[s], ue[s][:, 1:NT + 1], w1, acc[s], ALU.mult, ALU.add, [t_ue[s], t_cv, t_acc[s]], [t_acc[s]])
                stt(acc[s], ue[s][:, 0:NT], w0, acc[s], ALU.mult, ALU.add, [t_ue[s], t_cv, t_acc[s]], [t_acc[s]])
                gemm(psb[pbk][:, :], t_ps[pbk], [("win", O_CB + c * 128, 0)], hTk, lambda k: t_hT[k])
                tt(vT[:, c * NT:(c + 1) * NT], acc[s], psb[pbk][:, :], ALU.mult, [t_acc[s], t_ps[pbk]], [t_v[c]])
            if not full:
                return
            sig = [arena_view(46080 + j * 2048, NT, F32) for j in range(2)]
            t_sig = [Tile("sig0"), Tile("sig1")]
            mao = [arena_view(50176 + j * 2048, NT, F32) for j in range(2)]
            t_mao = [Tile("mao0"), Tile("mao1")]
            for c in range(32):
                pg, py = (0, 1) if c % 2 == 0 else (2, 3)
                s = c % 2
                gemm(psb[pg][:, :], t_ps[pg], [("win", O_GA + c * 128, 0)], hTk, lambda k: t_hT[k])
                act(sig[s], psb[pg][:, :], AF.Sigmoid, [t_ps[pg], t_cv], [t_sig[s]], bias=cv[:, CV_GB + c:CV_GB + c + 1])
                gemm(psb[py][:, :], t_ps[py], [("sco", c * 128, 0)], lambda k: vT[:, k * NT:(k + 1) * NT], lambda k: t_v[k])
                tt(mao[s], sig[s], psb[py][:, :], ALU.mult, [t_sig[s], t_ps[py]], [t_mao[s]])
                dma("sp", mA_d[c], mao[s], [t_mao[s]], [t_mA[c]])


        class Alloc:
            def __init__(self):
                self.o = 0

            def __call__(self, nbytes, nelem, dt):
                o = self.o
                self.o += (nbytes + 63) // 64 * 64
                assert self.o <= XC_OFF, self.o
                return arena_view(o, nelem, dt)

        def bc(ap_sb, off, F, dims):
            raise NotImplementedError

        def phase_ssd(ti, full, first):
            al = Alloc()
            xs_tm = al(32768, 4 * 4096, BF16)
            BT = al(8192, 8 * NT, BF16)
            CT = al(8192, 8 * NT, BF16)
            B_tm = al(8192, 4 * 1024, BF16)
            dt = al(2048, 512, F32)
            der = [[al(512, 128, F32) for j in range(6)] for tc in range(4)]
            xe = [al(2304, NT + 3, F32) for j in range(2)]
            cacc = [al(2048, NT, F32) for j in range(2)]
            xsT = [al(1024, NT, BF16) for j in range(2)]
            Sg = al(4096, 1024, F32)
            Sbf = al(2048, 1024, BF16)
            cbT = al(512, 128, F32)
            Ls = [al(2048, 512, F32) for j in range(2)]
            scT = [al(1024, 512, BF16) for j in range(2)]
            xdt = al(2048, 1024, BF16)
            xdd = al(2048, 1024, BF16)
            yg = al(4096, 1024, F32)
            Dx = al(2048, 512, F32)
            t1 = al(2048, 512, F32)
            stg = [al(2048, 512, F32) for j in range(2)]
            tmp = [al(512, 128, F32) for j in range(4)]
            t_xs = [Tile("xs_tm%d" % c) for c in range(32)]
            t_BT = [Tile("BT%d" % g) for g in range(8)]
            t_CT = [Tile("CT%d" % g) for g in range(8)]
            t_Btm = [Tile("Btm%d" % g) for g in range(8)]
            t_dt = [Tile("dt%d" % tc) for tc in range(4)]
            t_der = [Tile("der%d" % tc) for tc in range(4)]
            t_xe = [Tile("xe0"), Tile("xe1")]
            t_cacc = [Tile("cacc0"), Tile("cacc1")]
            t_xsT = [Tile("xsT0"), Tile("xsT1")]
            t_Sg, t_Sbf, t_cbT = Tile("Sg"), Tile("Sbf"), Tile("cbT")
            t_Ls = [Tile("Ls0"), Tile("Ls1")]
            t_scT = [Tile("scT0"), Tile("scT1")]
            t_xdt, t_xdd, t_yg, t_Dx, t_t1 = Tile("xdt"), Tile("xdd"), Tile("yg"), Tile("Dx"), Tile("t1")
            t_stg = [Tile("stg0"), Tile("stg1")]
            t_tmp = Tile("tmp")
            cn = {"n": 0}

            def conv_chunk(c):
                n = cn["n"]
                cn["n"] += 1
                pp = n % 4
                s = n % 2
                gemm(psb[pp][:, :], t_ps[pp], [("win", O_XBC + c * 128, 0)], hTk, lambda k: t_hT[k])
                cp(xe[s][:, 0:3], xh[:, c * 3:c * 3 + 3], [t_xh[c]], [t_xe[s]])
                cp(xe[s][:, 3:NT + 3], psb[pp][:, :], [t_ps[pp], t_xe[s]], [t_xe[s]], eng="act")
                cp(xh[:, c * 3:c * 3 + 3], xe[s][:, NT:NT + 3], [t_xe[s]], [t_xh[c]])
                wk = [cv[:, CV_CW + k * 80 + c:CV_CW + k * 80 + c + 1] for k in range(4)]
                ts(cacc[s], xe[s][:, 3:NT + 3], wk[3], cv[:, CV_CB + c:CV_CB + c + 1], ALU.mult, ALU.add,
                   [t_xe[s], t_cv], [t_cacc[s]])
                for k in (2, 1, 0):
                    stt(cacc[s], xe[s][:, k:NT + k], wk[k], cacc[s], ALU.mult, ALU.add, [t_xe[s], t_cv, t_cacc[s]], [t_cacc[s]])
                return s

            def r4(a):
                return a.rearrange("p (t c) -> p t c", t=4)

            for c in range(64, 80):
                s = conv_chunk(c)
                if c < 72:
                    g = c - 64
                    act(BT[:, g * NT:(g + 1) * NT], cacc[s], AF.Silu, [t_cacc[s]], [t_BT[g]])
                    hb = c % 2
                    for tc in range(4):
                        tr(pstb[:, hb * 512 + tc * 128: hb * 512 + (tc + 1) * 128],
                           BT[:, g * NT + tc * 128: g * NT + (tc + 1) * 128], identb[:, :], [t_BT[g], t_const], [t_pstb[hb]])
                    cp(r4(B_tm)[:, :, g * 128:(g + 1) * 128], r4(pstb[:, hb * 512:(hb + 1) * 512]), [t_pstb[hb]], [t_Btm[g]])
                else:
                    g = c - 72
                    act(CT[:, g * NT:(g + 1) * NT], cacc[s], AF.Silu, [t_cacc[s]], [t_CT[g]])
            w, wt, kc = next_unit("win", O_DT, 0)
            for tc in range(4):
                pp = 4 + tc % 2
                for k in range(32):
                    mm(psb[pp][:, 0:128], hT[:, k * NT + tc * 128: k * NT + (tc + 1) * 128], w[:, k * 128:(k + 1) * 128],
                       k == 0, k == 31, [wt, t_hT[k]], [t_ps[pp]])
                d_ = dt[:, tc * 128:(tc + 1) * 128]
                rd, wr = [t_tmp], [t_tmp]
                tt(tmp[0], psb[pp][:, 0:128], dtb_b, ALU.add, [t_ps[pp], t_rb] + rd, wr)
                ts1(tmp[1], tmp[0], -1.0, ALU.mult, rd, wr)
                tt(tmp[1], tmp[1], tmp[0], ALU.min, rd, wr)
                act(tmp[2], tmp[1], AF.Exp, rd, wr)
                act(tmp[2], tmp[2], AF.Ln, rd, wr, bias=1.0)
                ts1(tmp[3], tmp[0], 0.0, ALU.max, rd, wr)
                tt(d_, tmp[3], tmp[2], ALU.add, rd, [t_tmp, t_dt[tc]])
                a_, acs, ea, dte, nacs, cdb = der[tc]
                dd = [t_der[tc]]
                tt(a_, d_, Ab[:, :], ALU.mult, [t_dt[tc], t_const], dd)
                p1, p2 = 0, 1
                mm(psb[p1][:, 0:128], tri[:, :], a_, True, True, dd + [t_const], [t_ps[p1]])
                mm(psb[p2][:, 0:128], onesf[:, :], a_, True, True, dd + [t_const], [t_ps[p2]])
                cp(acs, psb[p1][:, 0:128], [t_ps[p1]] + dd, dd, eng="act")
                act(ea, acs, AF.Exp, dd, dd)
                tt(dte, psb[p2][:, 0:128], acs, ALU.subtract, [t_ps[p2]] + dd, dd)
                act(dte, dte, AF.Exp, dd, dd)
                act(cdb, psb[p2][:, 0:128], AF.Exp, [t_ps[p2]] + dd, dd)
                ts1(nacs, acs, -1.0, ALU.mult, dd, dd)

            def b3(ap2d, n1, n2):
                a = ap2d
                return bass.AP(a.tensor, a.offset, [list(a.ap[0]), [a.ap[-1][0], n1], [0, n2]])

            def v3(ap2d, n1, n2):
                return ap2d.rearrange("p (a b) -> p a b", a=n1)

            def cb3(ap2d, n1):
                a = ap2d
                return bass.AP(a.tensor, a.offset, [list(a.ap[0]), [0, n1], list(a.ap[-1])])

            for half in range(2):
                for cl in range(32):
                    c = half * 32 + cl
                    s = conv_chunk(c)
                    act(xsT[s], cacc[s], AF.Silu, [t_cacc[s]], [t_xsT[s]])
                    hb = c % 2
                    for tc in range(4):
                        tr(pstb[:, hb * 512 + tc * 128: hb * 512 + (tc + 1) * 128], xsT[s][:, tc * 128:(tc + 1) * 128],
                           identb[:, :], [t_xsT[s], t_const], [t_pstb[hb]])
                    cp(r4(xs_tm)[:, :, cl * 128:(cl + 1) * 128], r4(pstb[:, hb * 512:(hb + 1) * 512]), [t_pstb[hb]], [t_xs[cl]])
                for gl in range(4):
                    g = half * 4 + gl
                    if first:
                        P.add("dve", lambda e: e.memset(Sg, 0.0), writes=[t_Sg])
                    else:
                        dma("sp", Sg, S_d[g], [t_S[g]], [t_Sg])
                    xs_tiles = t_xs[gl * 8:(gl + 1) * 8]
                    for tc in range(4):
                        a_, acs, ea, dte, nacs, cdb = der[tc]
                        dd = [t_der[tc]]
                        tok = slice(g * NT + tc * 128, g * NT + (tc + 1) * 128)
                        if full:
                            mm(psb[0][:, 0:128], BT[:, tok], CT[:, tok], True, True, [t_BT[g], t_CT[g]], [t_ps[0]])
                            cp(cbT, psb[0][:, 0:128], [t_ps[0]], [t_cbT], eng="act")
                            cp(Sbf, Sg, [t_Sg], [t_Sbf])
                            for h2 in range(2):
                                mm(psb[1 + h2][:, :], CT[:, tok], Sbf[:, h2 * 512:(h2 + 1) * 512], True, True,
                                   [t_CT[g], t_Sbf], [t_ps[1 + h2]])
                        xs_g = xs_tm[:, tc * 4096 + gl * 1024: tc * 4096 + (gl + 1) * 1024]
                        tt(v3(xdt, 16, 64), v3(xs_g, 16, 64), b3(dt[:, tc * 128 + g * 16: tc * 128 + (g + 1) * 16], 16, 64),
                           ALU.mult, xs_tiles + [t_dt[tc]], [t_xdt])
                        tt(v3(xdd, 16, 64), v3(xdt, 16, 64), b3(dte[:, g * 16:(g + 1) * 16], 16, 64), ALU.mult,
                           [t_xdt] + dd, [t_xdd])
                        for q4 in range(4 if full else 0):
                            s = q4 % 2
                            pl = 3 + s
                            for q in range(4):
                                h = g * 16 + q4 * 4 + q
                                mm(psb[pl][:, q * 128:(q + 1) * 128], a_[:, h:h + 1].to_broadcast([128, 128]), tri[:, :],
                                   True, False, dd + [t_const], [t_ps[pl]])
                                mm(psb[pl][:, q * 128:(q + 1) * 128], identf[:, :], maskf[:, :], False, True, [t_const], [t_ps[pl]])
                            for q in range(4):
                                h = g * 16 + q4 * 4 + q
                                act(Ls[s][:, q * 128:(q + 1) * 128], psb[pl][:, q * 128:(q + 1) * 128], AF.Exp,
                                    [t_ps[pl]] + dd, [t_Ls[s]], bias=nacs[:, h:h + 1])
                            tt(v3(scT[s], 4, 128), v3(Ls[s], 4, 128), cb3(cbT, 4), ALU.mult, [t_Ls[s], t_cbT], [t_scT[s]])
                            for q in range(4):
                                hl = q4 * 4 + q
                                py = 5 + hl // 8
                                mm(psb[py][:, (hl % 8) * 64:(hl % 8 + 1) * 64], scT[s][:, q * 128:(q + 1) * 128],
                                   xdt[:, hl * 64:(hl + 1) * 64], True, True, [t_scT[s], t_xdt], [t_ps[py]])
                        for h2 in range(2 if full else 0):
                            h0 = g * 16 + h2 * 8
                            tt(v3(t1, 8, 64), v3(psb[1 + h2][:, :], 8, 64), b3(ea[:, h0:h0 + 8], 8, 64), ALU.mult,
                               [t_ps[1 + h2]] + dd, [t_t1])
                            tt(v3(Dx, 8, 64), v3(xs_g[:, h2 * 512:(h2 + 1) * 512], 8, 64), b3(D_b[:, h0:h0 + 8], 8, 64), ALU.mult,
                               xs_tiles + [t_rb], [t_Dx])
                            tt(t1, t1, Dx, ALU.add, [t_t1, t_Dx], [t_t1])
                            tt(yg[:, h2 * 512:(h2 + 1) * 512], psb[5 + h2][:, :], t1, ALU.add, [t_ps[5 + h2], t_t1], [t_yg])
                        for h2 in range(2):
                            mm(psb[1 + h2][:, :], B_tm[:, tc * 1024 + g * 128: tc * 1024 + (g + 1) * 128],
                               xdd[:, h2 * 512:(h2 + 1) * 512], True, True, [t_Btm[g], t_xdd], [t_ps[1 + h2]])
                        tt(v3(Sg, 16, 64), v3(Sg, 16, 64), b3(cdb[:, g * 16:(g + 1) * 16], 16, 64), ALU.mult, [t_Sg] + dd, [t_Sg])
                        for h2 in range(2):
                            tt(Sg[:, h2 * 512:(h2 + 1) * 512], Sg[:, h2 * 512:(h2 + 1) * 512], psb[1 + h2][:, :], ALU.add,
                               [t_Sg, t_ps[1 + h2]], [t_Sg])
                        if full:
                            for c4 in range(2):
                                s = c4
                                for q in range(4):
                                    cc = c4 * 4 + q
                                    tr(psb[0][:, q * 128:(q + 1) * 128], yg[:, cc * 128:(cc + 1) * 128], identf[:, :],
                                       [t_yg, t_const], [t_ps[0]])
                                cp(stg[s], psb[0][:, :], [t_ps[0]], [t_stg[s]], eng="act")
                                c0 = g * 8 + c4 * 4
                                dma("sp", yT_d[c0:c0 + 4, :, tc * 128:(tc + 1) * 128].rearrange("c p t -> p c t"),
                                    r4(stg[s]), [t_stg[s]], t_yT[c0:c0 + 4])
                    dma("sp", S_d[g], Sg, [t_Sg], [t_S[g]])

        def phase_post():
            al = Alloc()
            ynT = al(65536, 64 * NT, BF16)
            mT = al(32768, 32 * NT, BF16)
            yz = al(16384, 8 * NT, F32)
            sz = [al(2048, NT, F32) for j in range(2)]
            t_yn = [Tile("yn%d" % c) for c in range(64)]
            t_m = [Tile("m%d" % c) for c in range(32)]
            t_yz = [Tile("yz%d" % c) for c in range(8)]
            t_sz = [Tile("sz0"), Tile("sz1")]
            for g in range(8):
                for cc in range(8):
                    c = g * 8 + cc
                    pp = c % 4
                    s = c % 2
                    gemm(psb[pp][:, :], t_ps[pp], [("win", O_Z + c * 128, 0)], hTk, lambda k: t_hT[k])
                    act(sz[s], psb[pp][:, :], AF.Silu, [t_ps[pp]], [t_sz[s]])
                    a, t = load_chunk(yT_d[c], t_yT[c])
                    tt(yz[:, cc * NT:(cc + 1) * NT], sz[s], a, ALU.mult, [t_sz[s], t], [t_yz[cc]])
                    s2 = cnt["sq"] % 2
                    cnt["sq"] += 1
                    act(sq[s2], yz[:, cc * NT:(cc + 1) * NT], AF.Square, [t_yz[cc]], [t_sq[s2]])
                    mm(psb[6][:, :], onesb[:, :], sq[s2], cc == 0, cc == 7, [t_sq[s2], t_const], [t_ps[6]])
                ts(rstd, psb[6][:, :], 1.0 / 1024, EPS, ALU.mult, ALU.add, [t_ps[6]], [t_rstd])
                act(rstd, rstd, AF.Sqrt, [t_rstd], [t_rstd])
                P.add("dve", lambda e: e.reciprocal(out=rstd, in_=rstd), reads=[t_rstd], writes=[t_rstd])
                for cc in range(8):
                    c = g * 8 + cc
                    stt(ynT[:, c * NT:(c + 1) * NT], yz[:, cc * NT:(cc + 1) * NT], cv[:, CV_SN + c:CV_SN + c + 1], rstd,
                        ALU.mult, ALU.mult, [t_yz[cc], t_rstd, t_cv], [t_yn[c]])
            sig = [yz[:, j * NT:(j + 1) * NT] for j in range(2)]
            tmpm = [yz[:, (2 + j) * NT:(3 + j) * NT] for j in range(2)]
            t_sig = [t_yz[0], t_yz[1]]
            t_tm = [t_yz[2], t_yz[3]]
            for c in range(32):
                pg, py = (0, 1) if c % 2 == 0 else (2, 3)
                s = c % 2
                gemm(psb[pg][:, :], t_ps[pg], [("win", O_GB + c * 128, 0)], hTk, lambda k: t_hT[k])
                act(sig[s], psb[pg][:, :], AF.Sigmoid, [t_ps[pg], t_cv], [t_sig[s]], bias=cv[:, CV_GB + 32 + c:CV_GB + 32 + c + 1])
                gemm(psb[py][:, :], t_ps[py], [("sso", c * 128, 0), ("sso", c * 128, 32)],
                     lambda k: ynT[:, k * NT:(k + 1) * NT], lambda k: t_yn[k])
                a, t = load_chunk(mA_d[c], t_mA[c])
                tt(tmpm[s], sig[s], psb[py][:, :], ALU.mult, [t_sig[s], t_ps[py]], [t_tm[s]])
                tt(mT[:, c * NT:(c + 1) * NT], tmpm[s], a, ALU.add, [t_tm[s], t], [t_m[c]])
            xo = [sz[0], sz[1]]
            t_xo = t_sz
            for c in range(32):
                pp = 4 + c % 2
                gemm(psb[pp][:, :], t_ps[pp], [("wo", c * 128, 0)], lambda k: mT[:, k * NT:(k + 1) * NT], lambda k: t_m[k])
                a, t = load_chunk(xT_d[c], t_xT[c])
                s = c % 2
                tt(xo[s], psb[pp][:, :], a, ALU.add, [t_ps[pp], t], [t_xo[s]])
                dma("sp", xT_d[c], xo[s], [t_xo[s]], [t_xT[c]])

        def dbg_dump(i):
            if debug:
                P.barrier()
                td = Tile("dbg%d" % i)
                t_out.append(td)
                dma("sp", dbg_d[i], xT_d, t_xT, [td])
                P.barrier()

        def phase_mask():
            buf = [arena_view(j * 4096, 1024, F32) for j in range(2)]
            t_buf = [Tile("mb0"), Tile("mb1")]
            for g in range(8):
                s = g % 2
                dma("sp", buf[s], S_d[g], [t_S[g]], [t_buf[s]])
                ts1(buf[s], buf[s], smk[:, 0:1], ALU.mult, [t_buf[s], t_rb], [t_buf[s]])
                dma("sp", S_d[g], buf[s], [t_buf[s]], [t_S[g]])
            ts1(xh[:, :], xh[:, :], smk[:, 0:1], ALU.mult, t_xh + [t_rb], t_xh)
            ts1(uh[:, :], uh[:, :], smk[:, 0:1], ALU.mult, t_uh + [t_rb], t_uh)

        nfull_seen = 0
        for ti, mode in enumerate(modes):
            full = mode == "full"
            if full and ti > 0 and modes[ti - 1].startswith("prefix"):
                P.barrier()
                phase_mask()
            wstate["cur"] = (ulists[mode], ubase[mode], 0)
            wstate["i"] = 0
            P.barrier()
            phase_load(ti)
            P.barrier()
            if upto <= 0:
                dbg_dump(0)
                break
            phase_norm(CV_N1)
            P.barrier()
            if upto <= 1:
                dbg_dump(0)
                break
            phase_ffn(1)
            if ti == 0:
                dbg_dump(0)
            if upto <= 2:
                break
            phase_norm(CV_NM)
            P.barrier()
            if mode != "prefix0":
                phase_convbranch(full)
                P.barrier()
            if upto <= 3:
                break
            phase_ssd(ti, full, ti == 0)
            if upto <= 4:
                P.barrier()
                td = Tile("dby")
                t_out.append(td)
                dma("sp", dby_d, yT_d, t_yT, [td])
                break
            if full:
                if debug and ti == 0:
                    P.barrier()
                    td = Tile("dby")
                    t_out.append(td)
                    dma("sp", dby_d, yT_d, t_yT, [td])
                P.barrier()
                phase_post()
                if ti == 0:
                    dbg_dump(1)
                if upto <= 5:
                    break
                phase_norm(CV_N2)
                P.barrier()
                phase_ffn(2)
                if ti == 0:
                    dbg_dump(2)
                if upto <= 6:
                    break
                P.barrier()
                phase_norm(CV_NF, final=True, ti_out=nfull_seen)
                nfull_seen += 1
            assert upto < 99 or wstate["i"] == len(ulists[mode]), (wstate["i"], len(ulists[mode]))
        P.emit(final_tiles=t_out)
    return nc, nu, ulists, ubase


def _colvec(v, nchunk):
    return np.ascontiguousarray(np.asarray(v, np.float32).reshape(nchunk, 128).T)


def make_cvec(inp):
    cvh = np.zeros((128, NCV), np.float32)
    cvh[:, CV_N1:CV_N1 + 32] = _colvec(inp["ffn1_norm"][0], 32)
    cvh[:, CV_NM:CV_NM + 32] = _colvec(inp["mix_norm"][0], 32)
    cvh[:, CV_N2:CV_N2 + 32] = _colvec(inp["ffn2_norm"][0], 32)
    cvh[:, CV_NF:CV_NF + 32] = _colvec(inp["final_norm"], 32)
    cvh[:, CV_GB:CV_GB + 64] = _colvec(inp["gate_bias"][0], 64)
    for k in range(3):
        cvh[:, CV_SCW + k * 32:CV_SCW + (k + 1) * 32] = _colvec(inp["sconv_w"][0, k], 32)
    for k in range(4):
        cvh[:, CV_CW + k * 80:CV_CW + (k + 1) * 80] = _colvec(inp["ssm_conv_w"][0, k], 80)
    cvh[:, CV_CB:CV_CB + 80] = _colvec(inp["ssm_conv_b"][0], 80)
    cvh[:, CV_SN:CV_SN + 64] = _colvec(inp["ssm_norm"][0], 64)
    rbh = np.zeros((128, 384), np.float32)
    rbh[:, 0:128] = np.asarray(inp["ssm_dt_bias"][0], np.float32)[None, :]
    rbh[:, 128:256] = np.asarray(inp["ssm_A_log"][0], np.float32)[None, :]
    rbh[:, 256:384] = np.asarray(inp["ssm_D"][0], np.float32)[None, :]
    return cvh, rbh


def make_wstream(inp, nu, ulists, ubase):
    W = {"g1": inp["ffn1_w_gate"][0], "u1": inp["ffn1_w_up"][0], "d1": inp["ffn1_w_down"][0],
         "g2": inp["ffn2_w_gate"][0], "u2": inp["ffn2_w_up"][0], "d2": inp["ffn2_w_down"][0],
         "win": inp["w_in"][0], "sco": inp["sconv_w_out"][0], "sso": inp["ssm_w_out"][0], "wo": inp["w_o"][0]}
    W = {k: np.asarray(v, np.float32) for k, v in W.items()}
    ws = np.zeros((nu, 128, 4096), np.float32)
    for m, ul in ulists.items():
        b = ubase[m]
        for i, (name, col0, k0, kc) in enumerate(ul):
            blk = W[name][k0 * 128:(k0 + kc) * 128, col0:col0 + 128]
            ws[b + i, :, :kc * 128] = blk.reshape(kc, 128, 128).transpose(1, 0, 2).reshape(128, kc * 128)
    return ws


_CACHE = {}


def kernel(**inp):
    x = np.asarray(inp["x"], np.float32)
    modes = ["prefix0", "prefix", "full", "full"]
    key = "p0p1f2"
    if key not in _CACHE:
        _CACHE[key] = build(modes)
    nc, nu, ulists, ubase = _CACHE[key]
    cvh, rbh = make_cvec(inp)
    ws = make_wstream(inp, nu, {"full": ulists["full"]}, ubase)
    half = SEQ // 2
    in_maps = []
    for core in range(8):
        b, second = core // 2, core % 2
        if second:
            xc_ = np.ascontiguousarray(x[b])
        else:
            xc_ = np.ascontiguousarray(np.concatenate([x[b, :half], x[b, :half]], axis=0))
        smh = np.full((128, 1), float(second), np.float32)
        in_maps.append({"x": xc_, "ws": ws, "cv": cvh, "rb": rbh, "sm": smh})
    res = run_bass_kernel_spmd(nc, in_maps, core_ids=list(range(8)))
    out = np.empty((4, SEQ, D), np.float32)
    for core in range(8):
        b, second = core // 2, core % 2
        out[b, second * half:(second + 1) * half] = np.asarray(res.results[core]["out"], np.float32)
    return out
```
